# Optimizing a Trainium2 kernel written in Bass

```python
import jax, jax.numpy as jnp
from jax import lax
import numpy as np

D_MODEL = 1024
BATCH = 8
SEQ = 2048
DEPTH = 4

CTX_LEN = 256
GRID_W = 64
N_MIXERS = 3
WIDTH = D_MODEL
POOL_WINDOWS = (2, 4, 8, 16)
N_POOL_GROUPS = 4
POOL_GROUP = WIDTH // N_POOL_GROUPS
HEAD_DIM = 64
N_HEADS = WIDTH // HEAD_DIM
WIN_ROWS_MAX = 8
WIN_COLS = 16
CONV_WIDTH = 3
EPS = 1e-6
N_POOL_LAYERS = (DEPTH + 2) // 3
N_NA_LAYERS = (DEPTH + 1) // 3
N_CONV_LAYERS = DEPTH // 3

kernel_name = "hybrid_pool_natten_shortconv_dit"


def _rmsnorm(x, g):
    xf = x.astype(jnp.float32)
    y = xf * lax.rsqrt(jnp.mean(xf * xf, axis=-1, keepdims=True) + EPS)
    return (y * g.astype(jnp.float32)).astype(x.dtype)


def _modulation(cond, w, b):
    m = jax.nn.silu(cond) @ w + b
    return jnp.split(m, 3, axis=-1)


def _centred_mean(u, w):
    b_, l_, c_ = u.shape
    cs = jnp.concatenate([jnp.zeros((b_, 1, c_), jnp.float32),
                          jnp.cumsum(u.astype(jnp.float32), axis=1)], axis=1)
    t = jnp.arange(l_)
    lo = jnp.clip(t - w // 2, 0, l_)
    hi = jnp.clip(t + w // 2, 0, l_)
    cnt = (hi - lo).astype(jnp.float32)
    return ((cs[:, hi] - cs[:, lo]) / cnt[None, :, None]).astype(u.dtype)


def _pool_mixer(h, w_in, w_grp, scale, w_out):
    u, g = jnp.split(h @ w_in, 2, axis=-1)
    b_, l_, _ = u.shape
    ug = u.reshape(b_, l_, N_POOL_GROUPS, POOL_GROUP)
    pooled = jnp.stack([_centred_mean(ug[:, :, i], w) for i, w in enumerate(POOL_WINDOWS)], axis=2)
    mixed = jnp.einsum('blgc,gcd->blgd', pooled - ug, w_grp).reshape(b_, l_, WIDTH)
    return (mixed * scale * jax.nn.silu(g)) @ w_out


def _dwconv3(y, w, b):
    out = lax.conv_general_dilated(y, w[:, None, :], window_strides=(1,), padding=((1, 1),),
                                   dimension_numbers=('NWC', 'WIO', 'NWC'),
                                   feature_group_count=WIDTH)
    return out + b


def _conv_mixer(h, w_in, conv_w, conv_b, w_out):
    bg, cg, v, g = jnp.split(h @ w_in, 4, axis=-1)
    y = bg * _dwconv3(cg * v, conv_w, conv_b)
    return (y * jax.nn.silu(g)) @ w_out


def _na_mixer(h, hc, w_in, rpb, w_out, need_ctx_out):
    b_, l_, _ = h.shape
    rows = l_ // GRID_W
    wr = min(WIN_ROWS_MAX, rows)
    q, k, v, g = jnp.split(h @ w_in, 4, axis=-1)
    q = q.reshape(b_, rows, GRID_W, N_HEADS, HEAD_DIM) * HEAD_DIM ** -0.5
    k = k.reshape(b_, rows, GRID_W, N_HEADS, HEAD_DIM)
    v = v.reshape(b_, rows, GRID_W, N_HEADS, HEAD_DIM)
    if need_ctx_out:
        qc, kc, vc, gc = jnp.split(hc @ w_in, 4, axis=-1)
    else:
        kc, vc = jnp.split(hc @ w_in[:, WIDTH:3 * WIDTH], 2, axis=-1)
    n_ctx = hc.shape[1]
    kc = kc.reshape(b_, n_ctx, N_HEADS, HEAD_DIM)
    vc = vc.reshape(b_, n_ctx, N_HEADS, HEAD_DIM)

    r = jnp.arange(rows)
    row_idx = jnp.clip(r - wr // 2, 0, rows - wr)[:, None] + jnp.arange(wr)[None, :]
    col = jnp.arange(GRID_W)
    col_idx = jnp.clip(col - WIN_COLS // 2, 0, GRID_W - WIN_COLS)[:, None] + jnp.arange(WIN_COLS)[None, :]
    sel = jax.nn.one_hot(col_idx, GRID_W, dtype=h.dtype)

    kb = k[:, row_idx]
    vb = v[:, row_idx]
    s_blk = jnp.einsum('brqhd,brikhd->bhrqik', q, kb)
    s_loc = jnp.einsum('bhrqik,qjk->bhrqij', s_blk, sel).astype(jnp.float32)
    dr_idx = row_idx - r[:, None] + WIN_ROWS_MAX - 1
    dc_idx = col_idx - col[:, None] + WIN_COLS - 1
    bias = rpb[:, dr_idx[:, None, :, None], dc_idx[None, :, None, :]].astype(jnp.float32)
    s_loc = s_loc + bias[None]
    s_ctx = jnp.einsum('brqhd,bchd->bhrqc', q, kc).astype(jnp.float32)
    n_loc = wr * WIN_COLS
    logits = jnp.concatenate([s_loc.reshape(b_, N_HEADS, rows, GRID_W, n_loc), s_ctx], axis=-1)
    p = jax.nn.softmax(logits, axis=-1).astype(v.dtype)
    p_loc = p[..., :n_loc].reshape(b_, N_HEADS, rows, GRID_W, wr, WIN_COLS)
    p_ctx = p[..., n_loc:]
    p_blk = jnp.einsum('bhrqij,qjk->bhrqik', p_loc, sel)
    o = jnp.einsum('bhrqik,brikhd->brqhd', p_blk, vb) + jnp.einsum('bhrqc,bchd->brqhd', p_ctx, vc)
    y = (o.reshape(b_, l_, WIDTH) * jax.nn.silu(g)) @ w_out

    yc = None
    if need_ctx_out:
        qc = qc.reshape(b_, n_ctx, N_HEADS, HEAD_DIM) * HEAD_DIM ** -0.5
        sc = jnp.einsum('bqhd,bkhd->bhqk', qc, kc).astype(jnp.float32)
        pc = jax.nn.softmax(sc, axis=-1).astype(vc.dtype)
        oc = jnp.einsum('bhqk,bkhd->bqhd', pc, vc).reshape(b_, n_ctx, WIDTH)
        yc = (oc * jax.nn.silu(gc)) @ w_out
    return y, yc


def setup_inputs(seed: int = 0) -> dict:
    key = jax.random.key(seed)
    ks = jax.random.split(key, 20)
    nrm = jax.random.normal
    d, w = D_MODEL, WIDTH
    return {
        "x": nrm(ks[0], (BATCH, SEQ, d), jnp.float32),
        "c": nrm(ks[1], (BATCH, d), jnp.float32),
        "ctx": nrm(ks[2], (BATCH, CTX_LEN, d), jnp.float32),
        "c_ctx": nrm(ks[3], (d,), jnp.float32),
        "norm_g": 1.0 + 0.02 * nrm(ks[4], (DEPTH, d), jnp.float32),
        "ada_w": 0.5 * d ** -0.5 * nrm(ks[5], (DEPTH, d, 3 * d), jnp.float32),
        "ada_b": 0.01 * nrm(ks[6], (DEPTH, 3 * d), jnp.float32),
        "pool_w_in": d ** -0.5 * nrm(ks[7], (N_POOL_LAYERS, d, 2 * w), jnp.float32),
        "pool_w_grp": POOL_GROUP ** -0.5 * nrm(ks[8], (N_POOL_LAYERS, N_POOL_GROUPS, POOL_GROUP, POOL_GROUP), jnp.float32),
        "pool_scale": 1.0 + 0.1 * nrm(ks[9], (N_POOL_LAYERS, w), jnp.float32),
        "pool_w_out": w ** -0.5 * nrm(ks[10], (N_POOL_LAYERS, w, d), jnp.float32),
        "na_w_in": d ** -0.5 * nrm(ks[11], (N_NA_LAYERS, d, 4 * w), jnp.float32),
        "na_rpb": 0.1 * nrm(ks[12], (N_NA_LAYERS, N_HEADS, 2 * WIN_ROWS_MAX - 1, 2 * WIN_COLS - 1), jnp.float32),
        "na_w_out": w ** -0.5 * nrm(ks[13], (N_NA_LAYERS, w, d), jnp.float32),
        "conv_w_in": d ** -0.5 * nrm(ks[14], (N_CONV_LAYERS, d, 4 * w), jnp.float32),
        "conv_dw": CONV_WIDTH ** -0.5 * nrm(ks[15], (N_CONV_LAYERS, CONV_WIDTH, w), jnp.float32),
        "conv_db": 0.01 * nrm(ks[16], (N_CONV_LAYERS, w), jnp.float32),
        "conv_w_out": w ** -0.5 * nrm(ks[17], (N_CONV_LAYERS, w, d), jnp.float32),
        "final_g": 1.0 + 0.02 * nrm(ks[18], (d,), jnp.float32),
    }


def reference(x, c, ctx, c_ctx, norm_g, ada_w, ada_b, pool_w_in, pool_w_grp, pool_scale, pool_w_out,
              na_w_in, na_rpb, na_w_out, conv_w_in, conv_dw, conv_db, conv_w_out, final_g):
    last_ctx_reader = max([i for i in range(DEPTH) if i % N_MIXERS == 1], default=-1)
    for i in range(DEPTH):
        kind, j = i % N_MIXERS, i // N_MIXERS
        update_ctx = i < last_ctx_reader
        shift, scale, gate = _modulation(c, ada_w[i], ada_b[i])
        hx = _rmsnorm(x, norm_g[i]) * (1.0 + scale[:, None]) + shift[:, None]
        if kind == 1 or update_ctx:
            cshift, cscale, cgate = _modulation(c_ctx, ada_w[i], ada_b[i])
            hc = _rmsnorm(ctx, norm_g[i]) * (1.0 + cscale) + cshift
        if kind == 0:
            yx = _pool_mixer(hx, pool_w_in[j], pool_w_grp[j], pool_scale[j], pool_w_out[j])
            if update_ctx:
                yc = _pool_mixer(hc, pool_w_in[j], pool_w_grp[j], pool_scale[j], pool_w_out[j])
        elif kind == 1:
            yx, yc = _na_mixer(hx, hc, na_w_in[j], na_rpb[j], na_w_out[j], update_ctx)
        else:
            yx = _conv_mixer(hx, conv_w_in[j], conv_dw[j], conv_db[j], conv_w_out[j])
            if update_ctx:
                yc = _conv_mixer(hc, conv_w_in[j], conv_dw[j], conv_db[j], conv_w_out[j])
        x = x + gate[:, None] * yx
        if update_ctx:
            ctx = ctx + cgate * yc
    return _rmsnorm(x, final_g)
```

```python
import contextlib
import numpy as np
import concourse.bass as bass
import concourse.mybir as mybir
from concourse.bass_utils import run_bass_kernel_spmd

F32 = mybir.dt.float32
BF16 = mybir.dt.bfloat16
ALU = mybir.AluOpType
AF = mybir.ActivationFunctionType

ENGS = ("pe", "act", "dve", "pool", "sp")
D = 1024
T = 2048
TC = 256
TT = T + TC
NSLOT = 3
EPS = 1e-6
POOL_WINDOWS = (2, 4, 8, 16)


class Sched:
    def __init__(self, nc):
        self.nc = nc
        self.ops = []
        self.last_w = {}
        self.readers = {}
        self.eng_ops = {e: [] for e in ENGS}

    def _add(self, eng, fn, reads, writes, dma_slot=None):
        idx = len(self.ops)
        deps = set()
        for k in reads:
            w = self.last_w.get(k)
            if w is not None:
                deps.add(w)
        for k in writes:
            w = self.last_w.get(k)
            if w is not None:
                deps.add(w)
            for r in self.readers.get(k, ()):
                deps.add(r)
        for k in writes:
            self.last_w[k] = idx
            self.readers[k] = []
        for k in reads:
            self.readers.setdefault(k, []).append(idx)
        deps.discard(idx)
        self.ops.append(dict(eng=eng, fn=fn, deps=deps, slot=dma_slot,
                             pos=len(self.eng_ops[eng])))
        self.eng_ops[eng].append(idx)
        return idx

    def op(self, eng, fn, reads=(), writes=()):
        return self._add(eng, fn, tuple(reads), tuple(writes))

    def dma(self, queue, out, in_, reads=(), writes=(), slot=None):
        fn = lambda e, o=out, i=in_: e.dma_start(out=o, in_=i)
        return self._add(queue, fn, tuple(reads), tuple(writes), dma_slot=slot)

    def barrier_all(self, skip=()):
        last = []
        for e in ENGS:
            real = [i for i in self.eng_ops[e] if self.ops[i]["fn"] is not None]
            if real:
                last.append(real[-1])
        for e in ENGS:
            if e in skip:
                continue
            idx = self._add(e, None, (), ())
            self.ops[idx]["deps"] = set(last)

    def wait_ops(self, eng, dep_ops):
        idx = self._add(eng, None, (), ())
        self.ops[idx]["deps"] = set(dep_ops)

    def _skip(self, o, od):
        if od["slot"] is not None:
            return False
        if od["eng"] != o["eng"]:
            return False
        if o["eng"] == "pe":
            return True
        if o["eng"] == "pool":
            return False
        return o["pos"] - od["pos"] > 2

    def emit(self, stack):
        nc = self.nc
        ops = self.ops
        need = [False] * len(ops)
        for o in ops:
            for d in o["deps"]:
                od = ops[d]
                if od["slot"] is None and not self._skip(o, od):
                    need[d] = True
        sem_eng = {e: stack.enter_context(nc.semaphore("s_" + e)) for e in ENGS}
        slot_sems, slot_cnt, sig = {}, {}, {}
        cnt = {e: 0 for e in ENGS}
        for i, o in enumerate(ops):
            if o["slot"] is not None:
                sname = "d_" + "_".join(str(x) for x in (o["slot"] if isinstance(o["slot"], tuple) else (o["slot"],)))
                if sname not in slot_sems:
                    slot_sems[sname] = stack.enter_context(nc.semaphore(sname))
                    slot_cnt[sname] = 0
                slot_cnt[sname] += 16
                sig[i] = (slot_sems[sname], slot_cnt[sname], sname)
            elif need[i]:
                cnt[o["eng"]] += 1
                sig[i] = (sem_eng[o["eng"]], cnt[o["eng"]], "s_" + o["eng"])
        block = stack.enter_context(nc.Block())

        def run(ename, e):
            waited = {}
            for i in self.eng_ops[ename]:
                o = ops[i]
                wl = {}
                for d in o["deps"]:
                    od = ops[d]
                    if self._skip(o, od):
                        continue
                    sem, val, nm = sig[d]
                    if wl.get(nm, (None, 0))[1] < val:
                        wl[nm] = (sem, val)
                for nm, (sem, val) in wl.items():
                    if waited.get(nm, 0) >= val:
                        continue
                    waited[nm] = val
                    e.wait_ge(sem, val)
                if o["fn"] is None:
                    continue
                ins = o["fn"](e)
                if o["slot"] is not None:
                    ins.then_inc(sig[i][0], 16)
                elif need[i]:
                    ins.then_inc(sig[i][0], 1)

        @block.tensor
        def _(e):
            run("pe", e)

        @block.scalar
        def _(e):
            run("act", e)

        @block.vector
        def _(e):
            run("dve", e)

        @block.gpsimd
        def _(e):
            run("pool", e)

        @block.sync
        def _(e):
            run("sp", e)


def _slabs_from_cols(w, col_groups):
    out = []
    wr = w.reshape(8, 128, -1)
    for cols in col_groups:
        parts = [wr[:, :, c * 128:(c + 1) * 128] for c in cols]
        sl = np.concatenate(parts, axis=2)
        out.append(np.ascontiguousarray(sl.transpose(1, 0, 2)))
    return out


def _layer_plan():
    return [("pool", 0), ("na", 0), ("conv", 0), ("pool", 1)]


def _ada_after(kind, n_units):
    if kind == "pool":
        return [2, 1, 2, 1]
    return [1, 1, 1, 1, 1, 1, 0, 0]


def _stream_plan(n_layers):
    plan = _layer_plan()[:n_layers]
    out = [("ada", 0, s6) for s6 in range(6)]
    for i, (kind, j) in enumerate(plan):
        nxt = i + 1 < len(plan)
        nu = 4 if kind == "pool" else 8
        aft = _ada_after(kind, nu)
        a = 0
        for u in range(nu):
            out.append(("win", kind, j, u))
            if nxt:
                for _ in range(aft[u]):
                    out.append(("ada", i + 1, a))
                    a += 1
        out.append(("wout", kind, j, 0))
        out.append(("wout", kind, j, 1))
    return out


def _build_wstream(inp, n_layers):
    cache = {}

    def get(desc):
        if desc[0] == "ada":
            key = ("ada", desc[1])
            if key not in cache:
                cache[key] = _slabs_from_cols(inp["ada_w"][desc[1]], [[4 * q, 4 * q + 1, 4 * q + 2, 4 * q + 3] for q in range(6)])
            return cache[key][desc[2]]
        if desc[0] == "grp":
            g = np.zeros((128, 8, 512), np.float32)
            wg = inp["pool_w_grp"][desc[1]]
            for gi in range(4):
                for cc in range(2):
                    g[:, gi * 2 + cc, 0:256] = wg[gi, cc * 128:(cc + 1) * 128, :]
            return g
        if desc[0] == "win":
            _, kind, j, u = desc
            key = ("win", kind, j)
            if key not in cache:
                if kind == "pool":
                    cache[key] = _slabs_from_cols(inp["pool_w_in"][j], [[2 * gi, 2 * gi + 1, 8 + 2 * gi, 8 + 2 * gi + 1] for gi in range(4)])
                else:
                    w = inp["na_w_in"][j] if kind == "na" else inp["conv_w_in"][j]
                    cache[key] = _slabs_from_cols(w, [[c, 8 + c, 16 + c, 24 + c] for c in range(8)])
            return cache[key][u]
        _, kind, j, h = desc
        wo = {"pool": inp["pool_w_out"], "na": inp["na_w_out"], "conv": inp["conv_w_out"]}[kind][j]
        return _slabs_from_cols(wo, [[0, 1, 2, 3], [4, 5, 6, 7]])[h]

    plan = _stream_plan(n_layers)
    return np.stack([get(d) for d in plan]).reshape(len(plan), 128, 4096)


def _band_consts():
    out = np.zeros((128, 4, 5, 128), np.float32)
    L = 512
    for wi, w in enumerate(POOL_WINDOWS):
        t = np.arange(L)
        lo = np.clip(t - w // 2, 0, L)
        hi = np.clip(t + w // 2, 0, L)
        cnt = (hi - lo).astype(np.float64)
        M = np.zeros((L, L), np.float64)
        for d in range(L):
            M[lo[d]:hi[d], d] = 1.0 / cnt[d]
            M[d, d] -= 1.0
        out[:, wi, 0, :] = M[0:128, 0:128]
        out[:, wi, 1, :] = M[128:256, 128:256]
        out[:, wi, 2, :] = M[L - 128:L, L - 128:L]
        out[:, wi, 3, :] = M[0:128, 128:256]
        out[:, wi, 4, :] = M[128:256, 0:128]
    return out


def _ebias(rpb):
    p = np.arange(128)
    a = p // 64
    kc = p % 64
    u = np.arange(16)
    c = np.arange(64)
    dr = 14 - u[None, :] + a[:, None]
    cs = np.clip(c - 8, 0, 48)
    colvalid = (kc[:, None] >= cs[None, :]) & (kc[:, None] < cs[None, :] + 16)
    dc = np.clip(kc[:, None] - c[None, :] + 15, 0, 30)
    out = np.full((8, 128, 2, 2, 16, 64), -200.0, np.float32)
    for tbl in range(2):
        drvalid = (dr >= 0) & (dr <= 14)
        if tbl == 1:
            drvalid &= (dr >= 3) & (dr <= 10)
        valid = drvalid[:, :, None] & colvalid[:, None, :]
        drc = np.clip(dr, 0, 14)
        for h in range(16):
            g = rpb[h][drc[:, :, None], dc[:, None, :]]
            out[h // 2, :, h % 2, tbl] = np.where(valid, g, np.float32(-200.0))
    return out.reshape(8, 128, 4096)


def _wgrp(inp):
    g = np.zeros((2, 128, 8, 256), np.float32)
    for j in range(2):
        wg = inp["pool_w_grp"][j]
        for gi in range(4):
            for cc in range(2):
                g[j, :, gi * 2 + cc, :] = wg[gi, cc * 128:(cc + 1) * 128, :]
    return g.reshape(2, 128, 2048)


def _vecs(inp):
    v = np.zeros((128, 184), np.float32)
    v[:, 176:184] = inp["final_g"].reshape(8, 128).T
    for i in range(4):
        v[:, i * 8:(i + 1) * 8] = inp["norm_g"][i].reshape(8, 128).T
        v[:, 32 + i * 24:32 + (i + 1) * 24] = inp["ada_b"][i].reshape(24, 128).T
    for j in range(2):
        v[:, 128 + j * 8:128 + (j + 1) * 8] = inp["pool_scale"][j].reshape(8, 128).T
    for tap in range(3):
        v[:, 144 + tap * 8:144 + (tap + 1) * 8] = inp["conv_dw"][0][tap].reshape(8, 128).T
    v[:, 168:176] = inp["conv_db"][0].reshape(8, 128).T
    return v


def _na_tiles(qb):
    rows = list(range(8 * qb, 8 * qb + 8))
    need = {}
    for r in rows:
        if r <= 4:
            kts, tbl = range(0, 4), 0
        elif r >= 28:
            kts, tbl = range(12, 16), 0
        else:
            kts, tbl = range(-((-(r - 5)) // 2), (r + 3) // 2 + 1), 1
        for kt in kts:
            need.setdefault(kt, []).append((r, tbl))
    out = []
    for kt in sorted(need):
        lst = need[kt]
        rs = [r for r, _ in lst]
        assert rs == list(range(rs[0], rs[-1] + 1))
        runs = []
        start = 0
        for k in range(1, len(lst) + 1):
            if k == len(lst) or lst[k][1] != lst[start][1]:
                ra, rb, tbl = lst[start][0], lst[k - 1][0], lst[start][1]
                u0 = ra - 2 * kt + 7
                assert 0 <= u0 and u0 + (rb - ra + 1) <= 16
                runs.append((tbl, u0, (ra - 8 * qb) * 64, (rb - 8 * qb + 1) * 64))
                start = k
        out.append((kt, (rs[0] - 8 * qb) * 64, (rs[-1] - 8 * qb + 1) * 64, runs))
    return out


def build(n_layers=4, n_slabs=58, stage=99):
    n_slabs = _n_slabs(n_layers)
    nc = bass.Bass("TRN2", target_bir_lowering=False)
    x_d = nc.dram_tensor("xT", [128, 8 * T], F32, kind="ExternalInput").ap().rearrange("p (a b) -> p a b", b=T)
    ctx_d = nc.dram_tensor("ctxT", [128, 8 * TC], F32, kind="ExternalInput").ap().rearrange("p (a b) -> p a b", b=TC)
    cond_d = nc.dram_tensor("cond", [128, 16], F32, kind="ExternalInput").ap()
    vecs_d = nc.dram_tensor("vecs", [128, 184], F32, kind="ExternalInput").ap()
    ident_d = nc.dram_tensor("ident", [128, 128], F32, kind="ExternalInput").ap()
    band_d = nc.dram_tensor("band", [128, 4 * 5 * 128], F32, kind="ExternalInput").ap()
    w_d = nc.dram_tensor("wstream", [n_slabs, 128, 4096], F32, kind="ExternalInput").ap()
    eb_d = nc.dram_tensor("ebias", [8, 128, 4096], F32, kind="ExternalInput").ap()
    wgrp_d = nc.dram_tensor("wgrp", [2, 128, 2048], F32, kind="ExternalInput").ap()
    out_d = nc.dram_tensor("outT", [128, 8 * T], F32, kind="ExternalOutput").ap().rearrange("p (a b) -> p a b", b=T)

    st = contextlib.ExitStack()
    with st:
        sb = lambda n, shp, dt: st.enter_context(nc.sbuf_tensor("sb_" + n, shp, dt))
        xT = sb("xT", [128, 8, TT], F32)
        hT = sb("hT", [128, 8, TT], BF16)
        zT = sb("zT", [128, 8, TT], BF16)
        wring = [sb("wring%d" % i, [128, 8, 512], BF16) for i in range(NSLOT)]
        ident = sb("ident", [128, 128], F32)
        vecs = sb("vecs", [128, 184], F32)
        condf = sb("condf", [128, 16], F32)
        conds = sb("conds", [128, 16], BF16)
        ones_m = sb("ones_m", [128, 128], BF16)
        ones_1 = sb("ones_1", [128, 64], BF16)
        ident_b = sb("ident_b", [128, 128], BF16)
        epsb = sb("epsb", [128, 1], F32)
        warm = sb("warm", [128, 1], F32)
        mod = sb("mod", [128, 4, 24, 2], F32)
        modA = sb("modA", [128, 4, 8, 2], F32)
        tmpA = sb("tmpA", [128, 8, 2], F32)
        arena = sb("arena", [128, 9216], F32)
        psall = st.enter_context(nc.psum_tensor("psall", [128, 8, 512], F32))
        ps = [psall[:, i, :] for i in range(8)]

        zflat = zT[:, :, :].rearrange("p a b -> p (a b)").bitcast(F32)

        def carve(off_bytes, shape, dt, base=None):
            esz = 4 if dt == F32 else 2
            n = int(np.prod(shape))
            assert off_bytes % 4 == 0 and off_bytes + n * esz <= 9216 * 4, (off_bytes, shape)
            src = arena if base is None else base
            a = src[:, off_bytes // 4:(off_bytes + n * esz + 3) // 4]
            if dt != F32:
                a = a.bitcast(dt)
                a = a[:, 0:n]
            if len(shape) == 2:
                return a.rearrange("p (a b) -> p a b", b=shape[1])
            if len(shape) == 3:
                return a.rearrange("p (a b c) -> p a b c", b=shape[1], c=shape[2])
            return a

        s = Sched(nc)
        bank = [0]

        def nb():
            b = bank[0]
            bank[0] = (b + 1) % 8
            return b

        def mm(out, lhsT, rhs, start, stop, reads, writes):
            s.op("pe", lambda e, o=out, l=lhsT, r=rhs, a=start, z=stop:
                 e.matmul(o, lhsT=l, rhs=r, start=a, stop=z), reads, writes)

        wstate = dict(next_load=0, next_use=0)

        def issue_load():
            n = wstate["next_load"]
            if n >= n_slabs:
                return
            sl = n % NSLOT
            hold = [("xT", kc, 4) for kc in range(8)] if n in (7, 8) else []
            s.dma("pool", wring[sl][:], w_d[n].rearrange("p (a b) -> p a b", b=512),
                  reads=hold, writes=[("w", sl)], slot=("w", sl))
            wstate["next_load"] = n + 1

        splan = _stream_plan(n_layers)
        assert len(splan) == n_slabs

        def next_slab(expect, hold_prev=0):
            n = wstate["next_use"]
            assert splan[n][:len(expect)] == expect, (n, splan[n], expect)
            while wstate["next_load"] < min(n + NSLOT - hold_prev, n_slabs):
                issue_load()
            wstate["next_use"] = n + 1
            return wring[n % NSLOT], ("w", n % NSLOT)

        s.dma("sp", ident[:], ident_d, writes=["ident"], slot="c_ident")
        s.dma("sp", vecs[:], vecs_d, writes=["vecs"], slot="c_vecs")
        s.dma("sp", condf[:], cond_d, writes=["condf"], slot="c_cond")
        s.op("act", lambda e: e.activation(out=conds[:], in_=condf[:], func=AF.Silu), ["condf"], ["conds"])
        s.op("pool", lambda e: e.memset(ones_m[:], 1.0 / 1024), [], ["ones_m"])
        s.op("pool", lambda e: e.memset(ones_1[:], 1.0), [], ["ones_1"])
        s.op("dve", lambda e: e.tensor_copy(out=ident_b[:], in_=ident[:]), ["ident"], ["ident_b"])
        s.op("pool", lambda e: e.memset(epsb[:], EPS), [], ["epsb"])
        for _ in range(NSLOT):
            issue_load()

        plan = _layer_plan()[:n_layers]

        def blocks_for(with_ctx):
            bl = [(tb, tb * 512, 512, 0) for tb in range(4)]
            if with_ctx:
                bl.append((4, T, TC, 1))
            return bl

        def adaln_slab(i, s6):
            W, wk = next_slab(("ada", i, s6))
            b = nb()
            for q in range(4):
                for kc in range(8):
                    mm(ps[b][:, q * 2:q * 2 + 2], W[:, kc, q * 128:(q + 1) * 128], conds[:, kc * 2:(kc + 1) * 2], kc == 0, kc == 7,
                       [wk, "conds"], [("ps", b)])
            for j in range(2):
                s.op("dve", lambda e, o=mod[:, i, 4 * s6:4 * s6 + 4, j], p=ps[b][:, j:8:2],
                     v=vecs[:, 32 + i * 24 + 4 * s6:32 + i * 24 + 4 * s6 + 4]:
                     e.tensor_tensor(out=o, in0=p, in1=v, op=ALU.add), [("ps", b), "vecs"], [("mod", i)])

        def adaln_finish(i):
            mi = mod[:, i, :, :]
            for j in range(2):
                s.op("dve", lambda e, o=tmpA[:, :, j], a=mi[:, 8:16, j]:
                     e.tensor_scalar(out=o, in0=a, scalar1=1.0, scalar2=None, op0=ALU.add), [("mod", i)], ["tmpA"])
                s.op("dve", lambda e, o=modA[:, i, :, j], a=tmpA[:, :, j], v=vecs[:, i * 8:(i + 1) * 8]:
                     e.tensor_tensor(out=o, in0=a, in1=v, op=ALU.mult), ["tmpA", "vecs"], [("modA", i)])

        HOFF = 8192
        hbase = [None]

        def hphase_A(i, blk, part=0, kcs=0):
            (tb, t0, n, j) = blk
            sq = carve(HOFF if (hbase[0] is None or tb % 2 == 0) else 0, [8, 512], BF16, hbase[0])
            rstd = carve(HOFF + 8192 + (tb % 2) * 2048, [512], F32, hbase[0])
            rk = ("rstd", tb % 2)
            if part == 3:
                s.op("act", lambda e, o=sq[:, kcs, 0:n], i_=xT[:, kcs, t0:t0 + n]: e.activation(out=o, in_=i_, func=AF.Square),
                     [("xT", kcs, tb)], [("sq", kcs)])
                return
            if part in (0, 1):
                if hbase[0] is not None:
                    s.op("pool", lambda e, o=sq[:, :, 0:n], i_=xT[:, :, t0:t0 + n]: e.tensor_tensor(out=o, in0=i_, in1=i_, op=ALU.mult),
                         [("xT", kc, tb) for kc in range(8)], [("sqp", tb % 2, kc) for kc in range(8)])
                else:
                    s.op("act", lambda e, o=sq[:, :, 0:n], i_=xT[:, :, t0:t0 + n]: e.activation(out=o, in_=i_, func=AF.Square),
                         [("xT", kc, tb) for kc in range(8)], [("sq", kc) for kc in range(8)])
            if part == 1:
                return
            b = nb()
            for kc in range(8):
                sk = ("sq", kc) if hbase[0] is None else ("sqp", tb % 2, kc)
                mm(ps[b][:, 0:n], ones_m[:], sq[:, kc, 0:n], kc == 0, kc == 7, [sk, "ones_m"], [("ps", b)])
            s.op("act", lambda e, o=rstd[:, 0:n], p=ps[b][:, 0:n]: e.activation(out=o, in_=p, func=AF.Ln, bias=epsb[:, 0:1], scale=1.0),
                 [("ps", b), "epsb"], [rk])
            s.op("act", lambda e, o=rstd[:, 0:n]: e.activation(out=o, in_=o, func=AF.Exp, scale=-0.5), [rk], [rk])

        def hphase_B(i, blk, kc):
            (tb, t0, n, j) = blk
            rstd = carve(HOFF + 8192 + (tb % 2) * 2048, [512], F32, hbase[0])
            rk = ("rstd", tb % 2)
            tmpb = [carve(HOFF + 8192 + 4096 + k * 2048, [512], F32, hbase[0]) for k in range(2)]
            tm = tmpb[kc % 2]
            s.op("dve", lambda e, o=tm[:, 0:n], a=xT[:, kc, t0:t0 + n], sc=modA[:, i, kc, j:j + 1], r=rstd[:, 0:n]:
                 e.scalar_tensor_tensor(out=o, in0=a, scalar=sc, in1=r, op0=ALU.mult, op1=ALU.mult),
                 [("xT", kc, tb), ("modA", i), rk], [("tmpb", kc % 2)])
            s.op("act", lambda e, o=hT[:, kc, t0:t0 + n], a=tm[:, 0:n], bi=mod[:, i, kc, j:j + 1]:
                 e.activation(out=o, in_=a, func=AF.Identity, bias=bi, scale=1.0),
                 [("tmpb", kc % 2), ("mod", i)], [("hT", kc, tb)])

        fin = dict(out_ops=[])

        def final_B(blk, kc):
            (tb, t0, n, j) = blk
            rstd = carve(HOFF + 8192 + (tb % 2) * 2048, [512], F32)
            rk = ("rstd", tb % 2)
            h = kc // 4
            ost = carve(HOFF + 12288 + h * 8192, [4, 512], F32)
            s.op("dve", lambda e, o=ost[:, kc % 4, 0:n], a=xT[:, kc, t0:t0 + n], sc=vecs[:, 176 + kc:177 + kc], r=rstd[:, 0:n]:
                 e.scalar_tensor_tensor(out=o, in0=a, scalar=sc, in1=r, op0=ALU.mult, op1=ALU.mult),
                 [("xT", kc, tb), "vecs", rk], [("ost", h)])
            if kc % 4 == 3:
                fin["out_ops"].append(s.dma("sp", out_d[:, 4 * h:4 * h + 4, t0:t0 + n], ost[:, :, 0:n],
                                            reads=[("ost", h)], slot=("out", h)))

        def hphase_block(i, blk):
            hphase_A(i, blk)
            for kc in range(8):
                hphase_B(i, blk, kc)

        def wout_phase(i, blocks, kind, jj):
            last = (i == len(plan) - 1)
            Ws = [next_slab(("wout", kind, jj, 0)), next_slab(("wout", kind, jj, 1), hold_prev=1)]
            pend = None
            for blk in blocks:
                (tb, t0, n, j) = blk
                for oc in range(8):
                    W, wk = Ws[oc // 4]
                    ocl = oc % 4
                    b = nb()
                    for kc in range(8):
                        mm(ps[b][:, 0:n], W[:, kc, ocl * 128:(ocl + 1) * 128], zT[:, kc, t0:t0 + n], kc == 0, kc == 7,
                           [wk, ("zT", kc, tb)], [("ps", b)])
                    s.op("dve", lambda e, o=xT[:, oc, t0:t0 + n], p=ps[b][:, 0:n], g=mod[:, i, 16 + oc, j:j + 1]:
                         e.scalar_tensor_tensor(out=o, in0=p, scalar=g, in1=o, op0=ALU.mult, op1=ALU.add),
                         [("ps", b), ("mod", i), ("xT", oc, tb)], [("xT", oc, tb)])
                    Bop = (lambda bk, kc_: final_B(bk, kc_)) if last else (lambda bk, kc_: hphase_B(i + 1, bk, kc_))
                    if pend is not None:
                        if oc == 0:
                            hphase_A(i + 1, pend, part=2)
                        elif oc >= 2:
                            Bop(pend, oc - 2)
                    hphase_A(i + 1, blk, part=3, kcs=oc)
                if pend is not None:
                    Bop(pend, 6)
                    Bop(pend, 7)
                pend = blk
            hphase_A(i + 1, pend, part=2)
            for kc in range(8):
                Bop(pend, kc)

        hbase[0] = zflat
        pb = blocks_for(True)
        for (tb, t0, n, j) in pb:
            src = x_d[:, :, t0:t0 + n] if tb < 4 else ctx_d[:, :, :]
            s.dma("sp", xT[:, :, t0:t0 + n], src, writes=[("xT", kc, tb) for kc in range(8)], slot=("xld", tb))
        for s6 in range(6):
            adaln_slab(0, s6)
        adaln_finish(0)
        for k in range(5):
            hphase_A(0, pb[k])
            if k >= 1:
                for kc in range(8):
                    hphase_B(0, pb[k - 1], kc)
        for kc in range(8):
            hphase_B(0, pb[4], kc)
        hbase[0] = None

        def hkeys(tb):
            return [("hT", kc, tb) for kc in range(8)]

        def pool_layer(i, jj, blocks, nxt):
            ntile = 18 if len(blocks) == 5 else 16
            utok = carve(0, [18, 256], BF16)
            dT = carve(9216, [2, TT], BF16)
            sg = carve(18432, [2, TT], BF16)
            bandb = carve(27648, [20, 128], BF16)
            wg = carve(32768, [8, 256], BF16)
            s.dma("pool", bandb, band_d.rearrange("p (a b) -> p a b", b=128), writes=["bandb"], slot="bandb")
            s.dma("pool", wg, wgrp_d[jj].rearrange("p (a b) -> p a b", b=256), writes=["wg"], slot="wg")
            ada_n = 0
            for g in range(4):
                W, wk = next_slab(("win", "pool", jj, g))
                for t2 in range(0, ntile, 2):
                    b = nb()
                    for hf in range(2):
                        tt = t2 + hf
                        for kc in range(8):
                            mm(ps[b][:, hf * 256:(hf + 1) * 256], hT[:, kc, tt * 128:(tt + 1) * 128], W[:, kc, 0:256],
                               kc == 0, kc == 7, [wk, ("hT", kc, tt // 4)], [("ps", b)])
                    s.op("act", lambda e, o=utok[:, t2:t2 + 2, :], p=ps[b][:, :].rearrange("p (a b) -> p a b", b=256):
                         e.activation(out=o, in_=p, func=AF.Identity), [("ps", b)], [("utok", t2 // 4)])
                for cc in range(2):
                    for (tb, t0, n, j) in blocks:
                        b = nb()
                        for kc in range(8):
                            mm(ps[b][:, 0:n], W[:, kc, 256 + cc * 128:256 + (cc + 1) * 128], hT[:, kc, t0:t0 + n],
                               kc == 0, kc == 7, [wk, ("hT", kc, tb)], [("ps", b)])
                        s.op("act", lambda e, o=sg[:, cc, t0:t0 + n], p=ps[b][:, 0:n]: e.activation(out=o, in_=p, func=AF.Silu),
                             [("ps", b)], [("sg", cc, tb)])
                for cc in range(2):
                    for (tb, t0, n, j) in blocks:
                        b = nb()
                        for dl in range(n // 128):
                            td = t0 // 128 + dl
                            seg0, seg1 = (0, 16) if td < 16 else (16, 18)
                            srcs = []
                            if td - 1 >= seg0:
                                srcs.append((td - 1, 3))
                            srcs.append((td, 0 if td == seg0 else (2 if td == seg1 - 1 else 1)))
                            if td + 1 < seg1:
                                srcs.append((td + 1, 4))
                            for k, (ts, ty) in enumerate(srcs):
                                mm(ps[b][:, dl * 128:(dl + 1) * 128], utok[:, ts, cc * 128:(cc + 1) * 128],
                                   bandb[:, g * 5 + ty, :], k == 0, k == len(srcs) - 1,
                                   [("utok", ts // 4), "bandb"], [("ps", b)])
                        s.op("dve", lambda e, o=dT[:, cc, t0:t0 + n], p=ps[b][:, 0:n]: e.tensor_copy(out=o, in_=p),
                             [("ps", b)], [("dT", cc, tb)])
                for dc in range(2):
                    for (tb, t0, n, j) in blocks:
                        b = nb()
                        for cc in range(2):
                            mm(ps[b][:, 0:n], wg[:, g * 2 + cc, dc * 128:(dc + 1) * 128], dT[:, cc, t0:t0 + n],
                               cc == 0, cc == 1, ["wg", ("dT", cc, tb)], [("ps", b)])
                        ch = 2 * g + dc
                        s.op("dve", lambda e, o=zT[:, ch, t0:t0 + n], p=ps[b][:, 0:n], sc=vecs[:, 128 + jj * 8 + ch:128 + jj * 8 + ch + 1],
                             g_=sg[:, dc, t0:t0 + n]:
                             e.scalar_tensor_tensor(out=o, in0=p, scalar=sc, in1=g_, op0=ALU.mult, op1=ALU.mult),
                             [("ps", b), "vecs", ("sg", dc, tb)], [("zT", ch, tb)])
                if nxt:
                    for _ in range(_ada_after("pool", 4)[g]):
                        adaln_slab(i + 1, ada_n)
                        ada_n += 1
            if nxt:
                adaln_finish(i + 1)

        def conv_layer(i, blocks, nxt):
            vs = carve(0, [T], F32)
            pp = carve(8192, [T + 2], F32)
            acc = carve(16400, [T], F32)
            sg = carve(24592, [T], BF16)
            s.op("pool", lambda e: e.memset(pp[:, 0:1], 0.0), [], ["pp_pad"])
            s.op("pool", lambda e: e.memset(pp[:, T + 1:T + 2], 0.0), [], ["pp_pad"])
            ada_n = 0
            for c in range(8):
                W, wk = next_slab(("win", "conv", 0, c))

                def part_mm(part, tb, t0):
                    b = nb()
                    for kc in range(8):
                        mm(ps[b][:, :], W[:, kc, part * 128:(part + 1) * 128], hT[:, kc, t0:t0 + 512], kc == 0, kc == 7,
                           [wk, ("hT", kc, tb)], [("ps", b)])
                    return b
                for (tb, t0, n, j) in blocks:
                    b = part_mm(2, tb, t0)
                    s.op("act", lambda e, o=vs[:, t0:t0 + 512], p=ps[b][:, :]: e.activation(out=o, in_=p, func=AF.Identity),
                         [("ps", b)], [("vs", tb)])
                for (tb, t0, n, j) in blocks:
                    b = part_mm(1, tb, t0)
                    s.op("dve", lambda e, o=pp[:, 1 + t0:1 + t0 + 512], p=ps[b][:, :], v=vs[:, t0:t0 + 512]:
                         e.tensor_tensor(out=o, in0=p, in1=v, op=ALU.mult), [("ps", b), ("vs", tb)], ["pp"])
                w0 = vecs[:, 144 + c:144 + c + 1]
                w1 = vecs[:, 152 + c:152 + c + 1]
                w2 = vecs[:, 160 + c:160 + c + 1]
                bb = vecs[:, 168 + c:168 + c + 1]
                s.op("pool", lambda e, a=w1, b_=bb: e.tensor_scalar(out=acc[:, :], in0=pp[:, 1:T + 1], scalar1=a, scalar2=b_,
                                                                   op0=ALU.mult, op1=ALU.add), ["pp", "pp_pad", "vecs"], ["acc"])
                s.op("dve", lambda e, a=w0: e.scalar_tensor_tensor(out=acc[:, :], in0=pp[:, 0:T], scalar=a, in1=acc[:, :],
                                                                   op0=ALU.mult, op1=ALU.add), ["pp", "pp_pad", "acc"], ["acc"])
                s.op("dve", lambda e, a=w2: e.scalar_tensor_tensor(out=acc[:, :], in0=pp[:, 2:T + 2], scalar=a, in1=acc[:, :],
                                                                   op0=ALU.mult, op1=ALU.add), ["pp", "pp_pad", "acc"], ["acc"])
                for (tb, t0, n, j) in blocks:
                    b = part_mm(3, tb, t0)
                    s.op("act", lambda e, o=sg[:, t0:t0 + 512], p=ps[b][:, :]: e.activation(out=o, in_=p, func=AF.Silu),
                         [("ps", b)], [("sg", tb)])
                for (tb, t0, n, j) in blocks:
                    b = part_mm(0, tb, t0)
                    s.op("dve", lambda e, o=acc[:, t0:t0 + 512], p=ps[b][:, :]: e.tensor_tensor(out=o, in0=p, in1=o, op=ALU.mult),
                         [("ps", b), "acc"], [("y", tb)])
                    s.op("pool", lambda e, o=zT[:, c, t0:t0 + 512], a=acc[:, t0:t0 + 512], g_=sg[:, t0:t0 + 512]:
                         e.tensor_tensor(out=o, in0=a, in1=g_, op=ALU.mult), [("y", tb), ("sg", tb)], [("zT", c, tb), "acc"])
                if nxt:
                    for _ in range(_ada_after("conv", 8)[c]):
                        adaln_slab(i + 1, ada_n)
                        ada_n += 1
            if nxt:
                adaln_finish(i + 1)

        def na_layer(i, blocks, nxt):
            qT = carve(0, [T], BF16)
            kT = carve(4096, [TT], BF16)
            Vt = carve(8704, [18, 128], BF16)
            sg = carve(13312, [T], BF16)
            EB = carve(17408, [2, 2, 1024], BF16)
            rl = [carve(25600 + k * 2048, [512], F32) for k in range(2)]
            Pb = [carve(29696 + k * 2048, [2, 512], BF16) for k in range(3)]
            tiles_by_qb = [_na_tiles(qb) for qb in range(4)]
            ada_n = 0
            for pr in range(8):
                W, wk = next_slab(("win", "na", 0, pr))
                s.dma("pool", EB, eb_d[pr].rearrange("p (a b c) -> p a b c", a=2, b=2), writes=["EB"], slot="EB")
                for (tb, t0, n, j) in blocks[:4]:
                    b = nb()
                    for kc in range(8):
                        mm(ps[b][:, :], W[:, kc, 0:128], hT[:, kc, t0:t0 + 512], kc == 0, kc == 7, [wk, ("hT", kc, tb)], [("ps", b)])
                    s.op("dve", lambda e, o=qT[:, t0:t0 + 512], p=ps[b][:, :]:
                         e.tensor_scalar(out=o, in0=p, scalar1=0.125, scalar2=None, op0=ALU.mult), [("ps", b)], [("qT", tb)])
                for (tb, t0, n, j) in blocks[:4]:
                    b = nb()
                    for kc in range(8):
                        mm(ps[b][:, :], W[:, kc, 384:512], hT[:, kc, t0:t0 + 512], kc == 0, kc == 7, [wk, ("hT", kc, tb)], [("ps", b)])
                    s.op("act", lambda e, o=sg[:, t0:t0 + 512], p=ps[b][:, :]: e.activation(out=o, in_=p, func=AF.Silu),
                         [("ps", b)], [("sg", tb)])
                s.op("act", lambda e: e.activation(out=warm[:, 0:1], in_=epsb[:, 0:1], func=AF.Exp), ["epsb"], ["warm"])
                for (tb, t0, n, j) in blocks:
                    b = nb()
                    for kc in range(8):
                        mm(ps[b][:, 0:n], W[:, kc, 128:256], hT[:, kc, t0:t0 + n], kc == 0, kc == 7, [wk, ("hT", kc, tb)], [("ps", b)])
                    s.op("act", lambda e, o=kT[:, t0:t0 + n], p=ps[b][:, 0:n]: e.activation(out=o, in_=p, func=AF.Identity),
                         [("ps", b)], [("kT", tb)])
                for t4 in range(0, 18, 4):
                    nt = min(4, 18 - t4)
                    b = nb()
                    for q in range(nt):
                        tt = t4 + q
                        for kc in range(8):
                            mm(ps[b][:, q * 128:(q + 1) * 128], hT[:, kc, tt * 128:(tt + 1) * 128], W[:, kc, 256:384],
                               kc == 0, kc == 7, [wk, ("hT", kc, tt // 4)], [("ps", b)])
                    s.op("dve", lambda e, o=Vt[:, t4:t4 + nt, :], p=ps[b][:, 0:nt * 128].rearrange("p (a b) -> p a b", b=128):
                         e.tensor_copy(out=o, in_=p), [("ps", b)], [("Vt", t4 // 4)])
                steps = []
                for qb in range(4):
                    tiles = [(16, 0, 512, []), (17, 0, 512, [])] + tiles_by_qb[qb]
                    for idx, (kt, c0, c1, runs) in enumerate(tiles):
                        steps.append((qb, kt, c0, c1, runs, idx == 0, idx == len(tiles) - 1))
                ns = len(steps)
                LOOK = 2
                for n in range(ns + LOOK):
                    if n < ns:
                        qb, kt, c0, c1, runs, first, last = steps[n]
                        q0 = qb * 512
                        nq = c1 - c0
                        P = Pb[n % 3]
                        pk = ("P", n % 3)
                        b0 = (4, 6)[n % 2]
                        kkey = ("kT", kt // 4)
                        nr = len(runs)
                        mm(ps[b0][:, 0:nq], kT[0:64, kt * 128:(kt + 1) * 128], qT[0:64, q0 + c0:q0 + c1], True, nr == 0,
                           [kkey, ("qT", qb)], [("ps", b0), ("ps", b0 + 1)])
                        mm(ps[b0 + 1][:, 0:nq], kT[64:128, kt * 128:(kt + 1) * 128], qT[64:128, q0 + c0:q0 + c1], True, nr == 0,
                           [kkey, ("qT", qb)], [("ps", b0), ("ps", b0 + 1)])
                        for ri, (tbl, u0, ca, cb) in enumerate(runs):
                            for hh in range(2):
                                for hf in range(2):
                                    mm(ps[b0 + hh][hf * 64:(hf + 1) * 64, ca - c0:cb - c0], ident_b[:, hf * 64:(hf + 1) * 64],
                                       EB[:, hh, tbl, u0 * 64:u0 * 64 + (cb - ca)],
                                       False, ri == nr - 1, ["EB", "ident_b"], [("ps", b0), ("ps", b0 + 1)])
                        s.op("act", lambda e, o=P[:, :, 0:nq], p=psall[:, b0:b0 + 2, 0:nq]: e.activation(out=o, in_=p, func=AF.Exp),
                             [("ps", b0), ("ps", b0 + 1)], [pk])
                    m_ = n - LOOK
                    if m_ >= 0:
                        qb_, kt_, c0_, c1_, runs_, first_, last_ = steps[m_]
                        P_ = Pb[m_ % 3]
                        pk_ = ("P", m_ % 3)
                        bo, bl = ((0, 1), (2, 3))[(pr * 4 + qb_) % 2]
                        vk = ("Vt", kt_ // 4)
                        nq_ = c1_ - c0_
                        mm(ps[bo][0:64, c0_:c1_], Vt[:, kt_, 0:64], P_[:, 0, 0:nq_], first_, last_, [vk, pk_], [("ps", bo)])
                        mm(ps[bo][64:128, c0_:c1_], Vt[:, kt_, 64:128], P_[:, 1, 0:nq_], first_, last_, [vk, pk_], [("ps", bo)])
                        mm(ps[bl][0:64, c0_:c1_], ones_1[:, :], P_[:, 0, 0:nq_], first_, last_, ["ones_1", pk_], [("ps", bl)])
                        mm(ps[bl][64:128, c0_:c1_], ones_1[:, :], P_[:, 1, 0:nq_], first_, last_, ["ones_1", pk_], [("ps", bl)])
                        if last_:
                            q0_ = qb_ * 512
                            rlb = rl[(pr * 4 + qb_) % 2]
                            rk = ("rl", (pr * 4 + qb_) % 2)
                            s.op("dve", lambda e, p=ps[bl][:, :], r=rlb: e.reciprocal(out=r[:, :], in_=p), [("ps", bl)], [rk])
                            s.op("dve", lambda e, p=ps[bo][:, :], r=rlb: e.tensor_tensor(out=r[:, :], in0=p, in1=r[:, :], op=ALU.mult),
                                 [("ps", bo), rk], [rk])
                            s.op("pool", lambda e, o=zT[:, pr, q0_:q0_ + 512], g_=sg[:, q0_:q0_ + 512], r=rlb:
                                 e.tensor_tensor(out=o, in0=r[:, :], in1=g_, op=ALU.mult), [rk, ("sg", qb_)], [("zT", pr, qb_)])
                bank[0] = 4
                if nxt:
                    for _ in range(_ada_after("na", 8)[pr]):
                        adaln_slab(i + 1, ada_n)
                        ada_n += 1
            if nxt:
                adaln_finish(i + 1)

        for i, (kind, jj) in enumerate(plan):
            blocks = blocks_for(i <= 1)
            nxt = i + 1 < len(plan)
            if kind == "pool":
                pool_layer(i, jj, blocks, nxt)
            elif kind == "na":
                na_layer(i, blocks, nxt)
            else:
                conv_layer(i, blocks, nxt)
            s.barrier_all(skip=("pe",))
            wb = list(blocks if i == 0 else blocks[:4])
            if i == len(plan) - 1:
                wb = wb[:3] + [(3, 1536, 256, 0), (3, 1792, 256, 0)]
            wout_phase(i, wb, kind, jj)
            s.barrier_all(skip=("pe",))
        out_ops = fin["out_ops"]
        s.wait_ops("sp", out_ops)
        s.emit(st)
    return nc


def _n_slabs(n_layers):
    return len(_stream_plan(n_layers))


def prep_inputs(inp, n_layers=4):
    inp = {k: np.asarray(v) for k, v in inp.items()}
    ws = _build_wstream(inp, n_layers)
    vecs = _vecs(inp)
    ident = np.eye(128, dtype=np.float32)
    band = _band_consts().reshape(128, -1)
    eb = _ebias(inp["na_rpb"][0])
    wgrp = _wgrp(inp)
    cctx = inp["c_ctx"].reshape(8, 128).T
    maps = []
    for b in range(8):
        cond = np.zeros((128, 16), np.float32)
        cond[:, 0::2] = inp["c"][b].reshape(8, 128).T
        cond[:, 1::2] = cctx
        xT_h = np.ascontiguousarray(inp["x"][b].T.reshape(8, 128, T).transpose(1, 0, 2)).reshape(128, 8 * T)
        cT_h = np.ascontiguousarray(inp["ctx"][b].T.reshape(8, 128, TC).transpose(1, 0, 2)).reshape(128, 8 * TC)
        maps.append({"wgrp": wgrp, "xT": xT_h, "ctxT": cT_h,
                     "cond": cond, "vecs": vecs, "ident": ident, "band": band,
                     "wstream": ws, "ebias": eb})
    return maps


def kernel(**inputs):
    maps = prep_inputs(inputs, 4)
    nc = build(4, _n_slabs(4))
    res = run_bass_kernel_spmd(nc, maps, core_ids=list(range(8)))
    outs = []
    for r in res.results:
        oT = np.asarray(r["outT"], dtype=np.float32).reshape(128, 8, T)
        outs.append(np.ascontiguousarray(oT.transpose(2, 1, 0).reshape(T, D)))
    return np.stack(outs, axis=0)
```

```python
import contextlib
import numpy as np
import concourse.bass as bass
import concourse.mybir as mybir
from concourse.bass_utils import run_bass_kernel_spmd

F32 = mybir.dt.float32
BF16 = mybir.dt.bfloat16
ALU = mybir.AluOpType
AF = mybir.ActivationFunctionType

ENGS = ("pe", "act", "dve", "pool", "sp")
D = 1024
T = 2048
TC = 256
TT = T + TC
NSLOT = 3
EPS = 1e-6
POOL_WINDOWS = (2, 4, 8, 16)


class Sched:
    def __init__(self, nc):
        self.nc = nc
        self.ops = []
        self.last_w = {}
        self.readers = {}
        self.eng_ops = {e: [] for e in ENGS}

    def _add(self, eng, fn, reads, writes, dma_slot=None):
        idx = len(self.ops)
        deps = set()
        for k in reads:
            w = self.last_w.get(k)
            if w is not None:
                deps.add(w)
        for k in writes:
            w = self.last_w.get(k)
            if w is not None:
                deps.add(w)
            for r in self.readers.get(k, ()):
                deps.add(r)
        for k in writes:
            self.last_w[k] = idx
            self.readers[k] = []
        for k in reads:
            self.readers.setdefault(k, []).append(idx)
        deps.discard(idx)
        self.ops.append(dict(eng=eng, fn=fn, deps=deps, slot=dma_slot,
                             pos=len(self.eng_ops[eng])))
        self.eng_ops[eng].append(idx)
        return idx

    def op(self, eng, fn, reads=(), writes=()):
        return self._add(eng, fn, tuple(reads), tuple(writes))

    def dma(self, queue, out, in_, reads=(), writes=(), slot=None):
        fn = lambda e, o=out, i=in_: e.dma_start(out=o, in_=i)
        return self._add(queue, fn, tuple(reads), tuple(writes), dma_slot=slot)

    def barrier_all(self, skip=()):
        last = []
        for e in ENGS:
            real = [i for i in self.eng_ops[e] if self.ops[i]["fn"] is not None]
            if real:
                last.append(real[-1])
        for e in ENGS:
            if e in skip:
                continue
            idx = self._add(e, None, (), ())
            self.ops[idx]["deps"] = set(last)

    def wait_ops(self, eng, dep_ops):
        idx = self._add(eng, None, (), ())
        self.ops[idx]["deps"] = set(dep_ops)

    def _skip(self, o, od):
        if od["slot"] is not None:
            return False
        if od["eng"] != o["eng"]:
            return False
        if o["eng"] == "pe":
            return True
        if o["eng"] == "pool":
            return False
        return o["pos"] - od["pos"] > 2

    def emit(self, stack):
        nc = self.nc
        ops = self.ops
        need = [False] * len(ops)
        for o in ops:
            for d in o["deps"]:
                od = ops[d]
                if od["slot"] is None and not self._skip(o, od):
                    need[d] = True
        sem_eng = {e: stack.enter_context(nc.semaphore("s_" + e)) for e in ENGS}
        slot_sems, slot_cnt, sig = {}, {}, {}
        cnt = {e: 0 for e in ENGS}
        for i, o in enumerate(ops):
            if o["slot"] is not None:
                sname = "d_" + "_".join(str(x) for x in (o["slot"] if isinstance(o["slot"], tuple) else (o["slot"],)))
                if sname not in slot_sems:
                    slot_sems[sname] = stack.enter_context(nc.semaphore(sname))
                    slot_cnt[sname] = 0
                slot_cnt[sname] += 16
                sig[i] = (slot_sems[sname], slot_cnt[sname], sname)
            elif need[i]:
                cnt[o["eng"]] += 1
                sig[i] = (sem_eng[o["eng"]], cnt[o["eng"]], "s_" + o["eng"])
        block = stack.enter_context(nc.Block())

        def run(ename, e):
            waited = {}
            for i in self.eng_ops[ename]:
                o = ops[i]
                wl = {}
                for d in o["deps"]:
                    od = ops[d]
                    if self._skip(o, od):
                        continue
                    sem, val, nm = sig[d]
                    if wl.get(nm, (None, 0))[1] < val:
                        wl[nm] = (sem, val)
                for nm, (sem, val) in wl.items():
                    if waited.get(nm, 0) >= val:
                        continue
                    waited[nm] = val
                    e.wait_ge(sem, val)
                if o["fn"] is None:
                    continue
                ins = o["fn"](e)
                if o["slot"] is not None:
                    ins.then_inc(sig[i][0], 16)
                elif need[i]:
                    ins.then_inc(sig[i][0], 1)

        @block.tensor
        def _(e):
            run("pe", e)

        @block.scalar
        def _(e):
            run("act", e)

        @block.vector
        def _(e):
            run("dve", e)

        @block.gpsimd
        def _(e):
            run("pool", e)

        @block.sync
        def _(e):
            run("sp", e)


def _slabs_from_cols(w, col_groups):
    out = []
    wr = w.reshape(8, 128, -1)
    for cols in col_groups:
        parts = [wr[:, :, c * 128:(c + 1) * 128] for c in cols]
        sl = np.concatenate(parts, axis=2)
        out.append(np.ascontiguousarray(sl.transpose(1, 0, 2)))
    return out


def _layer_plan():
    return [("pool", 0), ("na", 0), ("conv", 0), ("pool", 1)]


def _ada_after(kind, n_units):
    if kind == "pool":
        return [2, 1, 2, 1]
    return [1, 1, 1, 1, 1, 1, 0, 0]


def _stream_plan(n_layers):
    plan = _layer_plan()[:n_layers]
    out = [("ada", 0, s6) for s6 in range(6)]
    for i, (kind, j) in enumerate(plan):
        nxt = i + 1 < len(plan)
        nu = 4 if kind == "pool" else 8
        aft = _ada_after(kind, nu)
        a = 0
        for u in range(nu):
            out.append(("win", kind, j, u))
            if nxt:
                for _ in range(aft[u]):
                    out.append(("ada", i + 1, a))
                    a += 1
        out.append(("wout", kind, j, 0))
        out.append(("wout", kind, j, 1))
    return out


def _build_wstream(inp, n_layers):
    cache = {}

    def get(desc):
        if desc[0] == "ada":
            key = ("ada", desc[1])
            if key not in cache:
                cache[key] = _slabs_from_cols(inp["ada_w"][desc[1]], [[4 * q, 4 * q + 1, 4 * q + 2, 4 * q + 3] for q in range(6)])
            return cache[key][desc[2]]
        if desc[0] == "grp":
            g = np.zeros((128, 8, 512), np.float32)
            wg = inp["pool_w_grp"][desc[1]]
            for gi in range(4):
                for cc in range(2):
                    g[:, gi * 2 + cc, 0:256] = wg[gi, cc * 128:(cc + 1) * 128, :]
            return g
        if desc[0] == "win":
            _, kind, j, u = desc
            key = ("win", kind, j)
            if key not in cache:
                if kind == "pool":
                    cache[key] = _slabs_from_cols(inp["pool_w_in"][j], [[2 * gi, 2 * gi + 1, 8 + 2 * gi, 8 + 2 * gi + 1] for gi in range(4)])
                else:
                    w = inp["na_w_in"][j] if kind == "na" else inp["conv_w_in"][j]
                    cache[key] = _slabs_from_cols(w, [[c, 8 + c, 16 + c, 24 + c] for c in range(8)])
            return cache[key][u]
        _, kind, j, h = desc
        wo = {"pool": inp["pool_w_out"], "na": inp["na_w_out"], "conv": inp["conv_w_out"]}[kind][j]
        return _slabs_from_cols(wo, [[0, 1, 2, 3], [4, 5, 6, 7]])[h]

    plan = _stream_plan(n_layers)
    return np.stack([get(d) for d in plan]).reshape(len(plan), 128, 4096)


def _band_consts():
    out = np.zeros((128, 4, 5, 128), np.float32)
    L = 512
    for wi, w in enumerate(POOL_WINDOWS):
        t = np.arange(L)
        lo = np.clip(t - w // 2, 0, L)
        hi = np.clip(t + w // 2, 0, L)
        cnt = (hi - lo).astype(np.float64)
        M = np.zeros((L, L), np.float64)
        for d in range(L):
            M[lo[d]:hi[d], d] = 1.0 / cnt[d]
            M[d, d] -= 1.0
        out[:, wi, 0, :] = M[0:128, 0:128]
        out[:, wi, 1, :] = M[128:256, 128:256]
        out[:, wi, 2, :] = M[L - 128:L, L - 128:L]
        out[:, wi, 3, :] = M[0:128, 128:256]
        out[:, wi, 4, :] = M[128:256, 0:128]
    return out


def _ebias(rpb):
    p = np.arange(128)
    a = p // 64
    kc = p % 64
    u = np.arange(16)
    c = np.arange(64)
    dr = 14 - u[None, :] + a[:, None]
    cs = np.clip(c - 8, 0, 48)
    colvalid = (kc[:, None] >= cs[None, :]) & (kc[:, None] < cs[None, :] + 16)
    dc = np.clip(kc[:, None] - c[None, :] + 15, 0, 30)
    out = np.full((8, 128, 2, 2, 16, 64), -200.0, np.float32)
    for tbl in range(2):
        drvalid = (dr >= 0) & (dr <= 14)
        if tbl == 1:
            drvalid &= (dr >= 3) & (dr <= 10)
        valid = drvalid[:, :, None] & colvalid[:, None, :]
        drc = np.clip(dr, 0, 14)
        for h in range(16):
            g = rpb[h][drc[:, :, None], dc[:, None, :]]
            out[h // 2, :, h % 2, tbl] = np.where(valid, g, np.float32(-200.0))
    return out.reshape(8, 128, 4096)


def _wgrp(inp):
    g = np.zeros((2, 128, 8, 256), np.float32)
    for j in range(2):
        wg = inp["pool_w_grp"][j]
        for gi in range(4):
            for cc in range(2):
                g[j, :, gi * 2 + cc, :] = wg[gi, cc * 128:(cc + 1) * 128, :]
    return g.reshape(2, 128, 2048)


def _vecs(inp):
    v = np.zeros((128, 184), np.float32)
    v[:, 176:184] = inp["final_g"].reshape(8, 128).T
    for i in range(4):
        v[:, i * 8:(i + 1) * 8] = inp["norm_g"][i].reshape(8, 128).T
        v[:, 32 + i * 24:32 + (i + 1) * 24] = inp["ada_b"][i].reshape(24, 128).T
    for j in range(2):
        v[:, 128 + j * 8:128 + (j + 1) * 8] = inp["pool_scale"][j].reshape(8, 128).T
    for tap in range(3):
        v[:, 144 + tap * 8:144 + (tap + 1) * 8] = inp["conv_dw"][0][tap].reshape(8, 128).T
    v[:, 168:176] = inp["conv_db"][0].reshape(8, 128).T
    return v


def _na_tiles(qb):
    rows = list(range(8 * qb, 8 * qb + 8))
    need = {}
    for r in rows:
        if r <= 4:
            kts, tbl = range(0, 4), 0
        elif r >= 28:
            kts, tbl = range(12, 16), 0
        else:
            kts, tbl = range(-((-(r - 5)) // 2), (r + 3) // 2 + 1), 1
        for kt in kts:
            need.setdefault(kt, []).append((r, tbl))
    out = []
    for kt in sorted(need):
        lst = need[kt]
        rs = [r for r, _ in lst]
        assert rs == list(range(rs[0], rs[-1] + 1))
        runs = []
        start = 0
        for k in range(1, len(lst) + 1):
            if k == len(lst) or lst[k][1] != lst[start][1]:
                ra, rb, tbl = lst[start][0], lst[k - 1][0], lst[start][1]
                u0 = ra - 2 * kt + 7
                assert 0 <= u0 and u0 + (rb - ra + 1) <= 16
                runs.append((tbl, u0, (ra - 8 * qb) * 64, (rb - 8 * qb + 1) * 64))
                start = k
        out.append((kt, (rs[0] - 8 * qb) * 64, (rs[-1] - 8 * qb + 1) * 64, runs))
    return out


def build(n_layers=4, n_slabs=58, stage=99):
    n_slabs = _n_slabs(n_layers)
    nc = bass.Bass("TRN2", target_bir_lowering=False)
    x_d = nc.dram_tensor("xT", [128, 8 * T], F32, kind="ExternalInput").ap().rearrange("p (a b) -> p a b", b=T)
    ctx_d = nc.dram_tensor("ctxT", [128, 8 * TC], F32, kind="ExternalInput").ap().rearrange("p (a b) -> p a b", b=TC)
    cond_d = nc.dram_tensor("cond", [128, 16], F32, kind="ExternalInput").ap()
    vecs_d = nc.dram_tensor("vecs", [128, 184], F32, kind="ExternalInput").ap()
    ident_d = nc.dram_tensor("ident", [128, 128], F32, kind="ExternalInput").ap()
    band_d = nc.dram_tensor("band", [128, 4 * 5 * 128], F32, kind="ExternalInput").ap()
    w_d = nc.dram_tensor("wstream", [n_slabs, 128, 4096], F32, kind="ExternalInput").ap()
    eb_d = nc.dram_tensor("ebias", [8, 128, 4096], F32, kind="ExternalInput").ap()
    wgrp_d = nc.dram_tensor("wgrp", [2, 128, 2048], F32, kind="ExternalInput").ap()
    out_d = nc.dram_tensor("outT", [128, 8 * T], F32, kind="ExternalOutput").ap().rearrange("p (a b) -> p a b", b=T)

    st = contextlib.ExitStack()
    with st:
        sb = lambda n, shp, dt: st.enter_context(nc.sbuf_tensor("sb_" + n, shp, dt))
        xT = sb("xT", [128, 8, TT], F32)
        hT = sb("hT", [128, 8, TT], BF16)
        zT = sb("zT", [128, 8, TT], BF16)
        wring = [sb("wring%d" % i, [128, 8, 512], BF16) for i in range(NSLOT)]
        ident = sb("ident", [128, 128], F32)
        vecs = sb("vecs", [128, 184], F32)
        condf = sb("condf", [128, 16], F32)
        conds = sb("conds", [128, 16], BF16)
        ones_m = sb("ones_m", [128, 128], BF16)
        ones_1 = sb("ones_1", [128, 64], BF16)
        ident_b = sb("ident_b", [128, 128], BF16)
        epsb = sb("epsb", [128, 1], F32)
        warm = sb("warm", [128, 1], F32)
        mod = sb("mod", [128, 4, 24, 2], F32)
        modA = sb("modA", [128, 4, 8, 2], F32)
        tmpA = sb("tmpA", [128, 8, 2], F32)
        arena = sb("arena", [128, 9216], F32)
        psall = st.enter_context(nc.psum_tensor("psall", [128, 8, 512], F32))
        ps = [psall[:, i, :] for i in range(8)]

        zflat = zT[:, :, :].rearrange("p a b -> p (a b)").bitcast(F32)

        def carve(off_bytes, shape, dt, base=None):
            esz = 4 if dt == F32 else 2
            n = int(np.prod(shape))
            assert off_bytes % 4 == 0 and off_bytes + n * esz <= 9216 * 4, (off_bytes, shape)
            src = arena if base is None else base
            a = src[:, off_bytes // 4:(off_bytes + n * esz + 3) // 4]
            if dt != F32:
                a = a.bitcast(dt)
                a = a[:, 0:n]
            if len(shape) == 2:
                return a.rearrange("p (a b) -> p a b", b=shape[1])
            if len(shape) == 3:
                return a.rearrange("p (a b c) -> p a b c", b=shape[1], c=shape[2])
            return a

        s = Sched(nc)
        bank = [0]

        def nb():
            b = bank[0]
            bank[0] = (b + 1) % 8
            return b

        def mm(out, lhsT, rhs, start, stop, reads, writes):
            s.op("pe", lambda e, o=out, l=lhsT, r=rhs, a=start, z=stop:
                 e.matmul(o, lhsT=l, rhs=r, start=a, stop=z), reads, writes)

        wstate = dict(next_load=0, next_use=0)

        def issue_load():
            n = wstate["next_load"]
            if n >= n_slabs:
                return
            sl = n % NSLOT
            s.dma("pool", wring[sl][:], w_d[n].rearrange("p (a b) -> p a b", b=512),
                  writes=[("w", sl)], slot=("w", sl))
            wstate["next_load"] = n + 1

        splan = _stream_plan(n_layers)
        assert len(splan) == n_slabs

        def next_slab(expect, hold_prev=0):
            n = wstate["next_use"]
            assert splan[n][:len(expect)] == expect, (n, splan[n], expect)
            while wstate["next_load"] < min(n + NSLOT - hold_prev, n_slabs):
                issue_load()
            wstate["next_use"] = n + 1
            return wring[n % NSLOT], ("w", n % NSLOT)

        s.dma("sp", ident[:], ident_d, writes=["ident"], slot="c_ident")
        s.dma("sp", vecs[:], vecs_d, writes=["vecs"], slot="c_vecs")
        s.dma("sp", condf[:], cond_d, writes=["condf"], slot="c_cond")
        s.op("act", lambda e: e.activation(out=conds[:], in_=condf[:], func=AF.Silu), ["condf"], ["conds"])
        s.op("pool", lambda e: e.memset(ones_m[:], 1.0 / 1024), [], ["ones_m"])
        s.op("pool", lambda e: e.memset(ones_1[:], 1.0), [], ["ones_1"])
        s.op("dve", lambda e: e.tensor_copy(out=ident_b[:], in_=ident[:]), ["ident"], ["ident_b"])
        s.op("pool", lambda e: e.memset(epsb[:], EPS), [], ["epsb"])
        for _ in range(NSLOT):
            issue_load()

        plan = _layer_plan()[:n_layers]

        def blocks_for(with_ctx):
            bl = [(tb, tb * 512, 512, 0) for tb in range(4)]
            if with_ctx:
                bl.append((4, T, TC, 1))
            return bl

        def adaln_slab(i, s6, hold_prev=0):
            W, wk = next_slab(("ada", i, s6), hold_prev=hold_prev)
            b = nb()
            for q in range(4):
                for kc in range(8):
                    mm(ps[b][:, q * 2:q * 2 + 2], W[:, kc, q * 128:(q + 1) * 128], conds[:, kc * 2:(kc + 1) * 2], kc == 0, kc == 7,
                       [wk, "conds"], [("ps", b)])
            for j in range(2):
                s.op("dve", lambda e, o=mod[:, i, 4 * s6:4 * s6 + 4, j], p=ps[b][:, j:8:2],
                     v=vecs[:, 32 + i * 24 + 4 * s6:32 + i * 24 + 4 * s6 + 4]:
                     e.tensor_tensor(out=o, in0=p, in1=v, op=ALU.add), [("ps", b), "vecs"], [("mod", i)])

        def adaln_finish(i):
            mi = mod[:, i, :, :]
            for j in range(2):
                s.op("dve", lambda e, o=tmpA[:, :, j], a=mi[:, 8:16, j]:
                     e.tensor_scalar(out=o, in0=a, scalar1=1.0, scalar2=None, op0=ALU.add), [("mod", i)], ["tmpA"])
                s.op("dve", lambda e, o=modA[:, i, :, j], a=tmpA[:, :, j], v=vecs[:, i * 8:(i + 1) * 8]:
                     e.tensor_tensor(out=o, in0=a, in1=v, op=ALU.mult), ["tmpA", "vecs"], [("modA", i)])

        HOFF = 8192
        hbase = [None]

        def hphase_A(i, blk, part=0, kcs=0):
            (tb, t0, n, j) = blk
            sq = carve(HOFF if (hbase[0] is None or tb % 2 == 0) else 0, [8, 512], BF16, hbase[0])
            rstd = carve(HOFF + 8192 + (tb % 2) * 2048, [512], F32, hbase[0])
            rk = ("rstd", tb % 2)
            if part == 3:
                s.op("act", lambda e, o=sq[:, kcs, 0:n], i_=xT[:, kcs, t0:t0 + n]: e.activation(out=o, in_=i_, func=AF.Square),
                     [("xT", kcs, tb)], [("sq", kcs)])
                return
            if part in (0, 1):
                if hbase[0] is not None:
                    s.op("pool", lambda e, o=sq[:, :, 0:n], i_=xT[:, :, t0:t0 + n]: e.tensor_tensor(out=o, in0=i_, in1=i_, op=ALU.mult),
                         [("xT", kc, tb) for kc in range(8)], [("sqp", tb % 2, kc) for kc in range(8)])
                else:
                    s.op("act", lambda e, o=sq[:, :, 0:n], i_=xT[:, :, t0:t0 + n]: e.activation(out=o, in_=i_, func=AF.Square),
                         [("xT", kc, tb) for kc in range(8)], [("sq", kc) for kc in range(8)])
            if part == 1:
                return
            b = nb()
            for kc in range(8):
                sk = ("sq", kc) if hbase[0] is None else ("sqp", tb % 2, kc)
                mm(ps[b][:, 0:n], ones_m[:], sq[:, kc, 0:n], kc == 0, kc == 7, [sk, "ones_m"], [("ps", b)])
            s.op("act", lambda e, o=rstd[:, 0:n], p=ps[b][:, 0:n]: e.activation(out=o, in_=p, func=AF.Ln, bias=epsb[:, 0:1], scale=1.0),
                 [("ps", b), "epsb"], [rk])
            s.op("act", lambda e, o=rstd[:, 0:n]: e.activation(out=o, in_=o, func=AF.Exp, scale=-0.5), [rk], [rk])

        def hphase_B(i, blk, kc):
            (tb, t0, n, j) = blk
            rstd = carve(HOFF + 8192 + (tb % 2) * 2048, [512], F32, hbase[0])
            rk = ("rstd", tb % 2)
            tmpb = [carve(HOFF + 8192 + 4096 + k * 2048, [512], F32, hbase[0]) for k in range(2)]
            tm = tmpb[kc % 2]
            s.op("dve", lambda e, o=tm[:, 0:n], a=xT[:, kc, t0:t0 + n], sc=modA[:, i, kc, j:j + 1], r=rstd[:, 0:n]:
                 e.scalar_tensor_tensor(out=o, in0=a, scalar=sc, in1=r, op0=ALU.mult, op1=ALU.mult),
                 [("xT", kc, tb), ("modA", i), rk], [("tmpb", kc % 2)])
            s.op("act", lambda e, o=hT[:, kc, t0:t0 + n], a=tm[:, 0:n], bi=mod[:, i, kc, j:j + 1]:
                 e.activation(out=o, in_=a, func=AF.Identity, bias=bi, scale=1.0),
                 [("tmpb", kc % 2), ("mod", i)], [("hT", kc, tb)])

        fin = dict(out_ops=[])

        def final_B(blk, kc):
            (tb, t0, n, j) = blk
            rstd = carve(HOFF + 8192 + (tb % 2) * 2048, [512], F32)
            rk = ("rstd", tb % 2)
            h = kc // 4
            ost = carve(HOFF + 12288 + h * 8192, [4, 512], F32)
            s.op("dve", lambda e, o=ost[:, kc % 4, 0:n], a=xT[:, kc, t0:t0 + n], sc=vecs[:, 176 + kc:177 + kc], r=rstd[:, 0:n]:
                 e.scalar_tensor_tensor(out=o, in0=a, scalar=sc, in1=r, op0=ALU.mult, op1=ALU.mult),
                 [("xT", kc, tb), "vecs", rk], [("ost", h)])
            if kc % 4 == 3:
                fin["out_ops"].append(s.dma("sp", out_d[:, 4 * h:4 * h + 4, t0:t0 + n], ost[:, :, 0:n],
                                            reads=[("ost", h)], slot=("out", h)))

        def hphase_block(i, blk):
            hphase_A(i, blk)
            for kc in range(8):
                hphase_B(i, blk, kc)

        def wout_phase(i, blocks, kind, jj):
            last = (i == len(plan) - 1)
            Ws = [next_slab(("wout", kind, jj, 0)), next_slab(("wout", kind, jj, 1), hold_prev=1)]
            pend = None
            for blk in blocks:
                (tb, t0, n, j) = blk
                for oc in range(8):
                    W, wk = Ws[oc // 4]
                    ocl = oc % 4
                    b = nb()
                    for kc in range(8):
                        mm(ps[b][:, 0:n], W[:, kc, ocl * 128:(ocl + 1) * 128], zT[:, kc, t0:t0 + n], kc == 0, kc == 7,
                           [wk, ("zT", kc, tb)], [("ps", b)])
                    s.op("dve", lambda e, o=xT[:, oc, t0:t0 + n], p=ps[b][:, 0:n], g=mod[:, i, 16 + oc, j:j + 1]:
                         e.scalar_tensor_tensor(out=o, in0=p, scalar=g, in1=o, op0=ALU.mult, op1=ALU.add),
                         [("ps", b), ("mod", i), ("xT", oc, tb)], [("xT", oc, tb)])
                    Bop = (lambda bk, kc_: final_B(bk, kc_)) if last else (lambda bk, kc_: hphase_B(i + 1, bk, kc_))
                    if pend is not None:
                        if oc == 0:
                            hphase_A(i + 1, pend, part=2)
                        elif oc >= 2:
                            Bop(pend, oc - 2)
                    hphase_A(i + 1, blk, part=3, kcs=oc)
                if pend is not None:
                    Bop(pend, 6)
                    Bop(pend, 7)
                pend = blk
            hphase_A(i + 1, pend, part=2)
            for kc in range(8):
                Bop(pend, kc)

        hbase[0] = zflat
        pb = blocks_for(True)
        for (tb, t0, n, j) in pb:
            src = x_d[:, :, t0:t0 + n] if tb < 4 else ctx_d[:, :, :]
            s.dma("sp", xT[:, :, t0:t0 + n], src, writes=[("xT", kc, tb) for kc in range(8)], slot=("xld", tb))
        for s6 in range(6):
            adaln_slab(0, s6, hold_prev=1 if s6 == 5 else 0)
        adaln_finish(0)
        for k in range(5):
            hphase_A(0, pb[k])
            if k >= 1:
                for kc in range(8):
                    hphase_B(0, pb[k - 1], kc)
        for kc in range(8):
            hphase_B(0, pb[4], kc)
        hbase[0] = None

        def hkeys(tb):
            return [("hT", kc, tb) for kc in range(8)]

        def pool_layer(i, jj, blocks, nxt):
            ntile = 18 if len(blocks) == 5 else 16
            utok = carve(0, [18, 256], BF16)
            dT = carve(9216, [2, TT], BF16)
            sg = carve(18432, [2, TT], BF16)
            bandb = carve(27648, [20, 128], BF16)
            wg = carve(32768, [8, 256], BF16)
            s.dma("pool", bandb, band_d.rearrange("p (a b) -> p a b", b=128), writes=["bandb"], slot="bandb")
            s.dma("pool", wg, wgrp_d[jj].rearrange("p (a b) -> p a b", b=256), writes=["wg"], slot="wg")
            ada_n = 0
            for g in range(4):
                W, wk = next_slab(("win", "pool", jj, g))
                for t2 in range(0, ntile, 2):
                    b = nb()
                    for hf in range(2):
                        tt = t2 + hf
                        for kc in range(8):
                            mm(ps[b][:, hf * 256:(hf + 1) * 256], hT[:, kc, tt * 128:(tt + 1) * 128], W[:, kc, 0:256],
                               kc == 0, kc == 7, [wk, ("hT", kc, tt // 4)], [("ps", b)])
                    s.op("act", lambda e, o=utok[:, t2:t2 + 2, :], p=ps[b][:, :].rearrange("p (a b) -> p a b", b=256):
                         e.activation(out=o, in_=p, func=AF.Identity), [("ps", b)], [("utok", t2 // 4)])
                for cc in range(2):
                    for (tb, t0, n, j) in blocks:
                        b = nb()
                        for kc in range(8):
                            mm(ps[b][:, 0:n], W[:, kc, 256 + cc * 128:256 + (cc + 1) * 128], hT[:, kc, t0:t0 + n],
                               kc == 0, kc == 7, [wk, ("hT", kc, tb)], [("ps", b)])
                        s.op("act", lambda e, o=sg[:, cc, t0:t0 + n], p=ps[b][:, 0:n]: e.activation(out=o, in_=p, func=AF.Silu),
                             [("ps", b)], [("sg", cc, tb)])
                for cc in range(2):
                    for (tb, t0, n, j) in blocks:
                        b = nb()
                        for dl in range(n // 128):
                            td = t0 // 128 + dl
                            seg0, seg1 = (0, 16) if td < 16 else (16, 18)
                            srcs = []
                            if td - 1 >= seg0:
                                srcs.append((td - 1, 3))
                            srcs.append((td, 0 if td == seg0 else (2 if td == seg1 - 1 else 1)))
                            if td + 1 < seg1:
                                srcs.append((td + 1, 4))
                            for k, (ts, ty) in enumerate(srcs):
                                mm(ps[b][:, dl * 128:(dl + 1) * 128], utok[:, ts, cc * 128:(cc + 1) * 128],
                                   bandb[:, g * 5 + ty, :], k == 0, k == len(srcs) - 1,
                                   [("utok", ts // 4), "bandb"], [("ps", b)])
                        s.op("dve", lambda e, o=dT[:, cc, t0:t0 + n], p=ps[b][:, 0:n]: e.tensor_copy(out=o, in_=p),
                             [("ps", b)], [("dT", cc, tb)])
                for dc in range(2):
                    for (tb, t0, n, j) in blocks:
                        b = nb()
                        for cc in range(2):
                            mm(ps[b][:, 0:n], wg[:, g * 2 + cc, dc * 128:(dc + 1) * 128], dT[:, cc, t0:t0 + n],
                               cc == 0, cc == 1, ["wg", ("dT", cc, tb)], [("ps", b)])
                        ch = 2 * g + dc
                        s.op("dve", lambda e, o=zT[:, ch, t0:t0 + n], p=ps[b][:, 0:n], sc=vecs[:, 128 + jj * 8 + ch:128 + jj * 8 + ch + 1],
                             g_=sg[:, dc, t0:t0 + n]:
                             e.scalar_tensor_tensor(out=o, in0=p, scalar=sc, in1=g_, op0=ALU.mult, op1=ALU.mult),
                             [("ps", b), "vecs", ("sg", dc, tb)], [("zT", ch, tb)])
                if nxt:
                    for _ in range(_ada_after("pool", 4)[g]):
                        adaln_slab(i + 1, ada_n)
                        ada_n += 1
            if nxt:
                adaln_finish(i + 1)

        def conv_layer(i, blocks, nxt):
            vs = carve(0, [T], F32)
            pp = carve(8192, [T + 2], F32)
            acc = carve(16400, [T], F32)
            sg = carve(24592, [T], BF16)
            s.op("pool", lambda e: e.memset(pp[:, 0:1], 0.0), [], ["pp_pad"])
            s.op("pool", lambda e: e.memset(pp[:, T + 1:T + 2], 0.0), [], ["pp_pad"])
            ada_n = 0
            for c in range(8):
                W, wk = next_slab(("win", "conv", 0, c))

                def part_mm(part, tb, t0):
                    b = nb()
                    for kc in range(8):
                        mm(ps[b][:, :], W[:, kc, part * 128:(part + 1) * 128], hT[:, kc, t0:t0 + 512], kc == 0, kc == 7,
                           [wk, ("hT", kc, tb)], [("ps", b)])
                    return b
                for (tb, t0, n, j) in blocks:
                    b = part_mm(2, tb, t0)
                    s.op("act", lambda e, o=vs[:, t0:t0 + 512], p=ps[b][:, :]: e.activation(out=o, in_=p, func=AF.Identity),
                         [("ps", b)], [("vs", tb)])
                for (tb, t0, n, j) in blocks:
                    b = part_mm(1, tb, t0)
                    s.op("dve", lambda e, o=pp[:, 1 + t0:1 + t0 + 512], p=ps[b][:, :], v=vs[:, t0:t0 + 512]:
                         e.tensor_tensor(out=o, in0=p, in1=v, op=ALU.mult), [("ps", b), ("vs", tb)], ["pp"])
                w0 = vecs[:, 144 + c:144 + c + 1]
                w1 = vecs[:, 152 + c:152 + c + 1]
                w2 = vecs[:, 160 + c:160 + c + 1]
                bb = vecs[:, 168 + c:168 + c + 1]
                s.op("pool", lambda e, a=w1, b_=bb: e.tensor_scalar(out=acc[:, :], in0=pp[:, 1:T + 1], scalar1=a, scalar2=b_,
                                                                   op0=ALU.mult, op1=ALU.add), ["pp", "pp_pad", "vecs"], ["acc"])
                s.op("dve", lambda e, a=w0: e.scalar_tensor_tensor(out=acc[:, :], in0=pp[:, 0:T], scalar=a, in1=acc[:, :],
                                                                   op0=ALU.mult, op1=ALU.add), ["pp", "pp_pad", "acc"], ["acc"])
                s.op("dve", lambda e, a=w2: e.scalar_tensor_tensor(out=acc[:, :], in0=pp[:, 2:T + 2], scalar=a, in1=acc[:, :],
                                                                   op0=ALU.mult, op1=ALU.add), ["pp", "pp_pad", "acc"], ["acc"])
                for (tb, t0, n, j) in blocks:
                    b = part_mm(3, tb, t0)
                    s.op("act", lambda e, o=sg[:, t0:t0 + 512], p=ps[b][:, :]: e.activation(out=o, in_=p, func=AF.Silu),
                         [("ps", b)], [("sg", tb)])
                for (tb, t0, n, j) in blocks:
                    b = part_mm(0, tb, t0)
                    s.op("dve", lambda e, o=acc[:, t0:t0 + 512], p=ps[b][:, :]: e.tensor_tensor(out=o, in0=p, in1=o, op=ALU.mult),
                         [("ps", b), "acc"], [("y", tb)])
                    s.op("pool", lambda e, o=zT[:, c, t0:t0 + 512], a=acc[:, t0:t0 + 512], g_=sg[:, t0:t0 + 512]:
                         e.tensor_tensor(out=o, in0=a, in1=g_, op=ALU.mult), [("y", tb), ("sg", tb)], [("zT", c, tb), "acc"])
                if nxt:
                    for _ in range(_ada_after("conv", 8)[c]):
                        adaln_slab(i + 1, ada_n)
                        ada_n += 1
            if nxt:
                adaln_finish(i + 1)

        def na_layer(i, blocks, nxt):
            qT = carve(0, [T], BF16)
            kT = carve(4096, [TT], BF16)
            Vt = carve(8704, [18, 128], BF16)
            sg = carve(13312, [T], BF16)
            EB = carve(17408, [2, 2, 1024], BF16)
            rl = [carve(25600 + k * 2048, [512], F32) for k in range(2)]
            Pb = [carve(29696 + k * 2048, [2, 512], BF16) for k in range(3)]
            tiles_by_qb = [_na_tiles(qb) for qb in range(4)]
            ada_n = 0
            for pr in range(8):
                W, wk = next_slab(("win", "na", 0, pr))
                s.dma("pool", EB, eb_d[pr].rearrange("p (a b c) -> p a b c", a=2, b=2), writes=["EB"], slot="EB")
                for (tb, t0, n, j) in blocks[:4]:
                    b = nb()
                    for kc in range(8):
                        mm(ps[b][:, :], W[:, kc, 0:128], hT[:, kc, t0:t0 + 512], kc == 0, kc == 7, [wk, ("hT", kc, tb)], [("ps", b)])
                    s.op("dve", lambda e, o=qT[:, t0:t0 + 512], p=ps[b][:, :]:
                         e.tensor_scalar(out=o, in0=p, scalar1=0.125, scalar2=None, op0=ALU.mult), [("ps", b)], [("qT", tb)])
                for (tb, t0, n, j) in blocks[:4]:
                    b = nb()
                    for kc in range(8):
                        mm(ps[b][:, :], W[:, kc, 384:512], hT[:, kc, t0:t0 + 512], kc == 0, kc == 7, [wk, ("hT", kc, tb)], [("ps", b)])
                    s.op("act", lambda e, o=sg[:, t0:t0 + 512], p=ps[b][:, :]: e.activation(out=o, in_=p, func=AF.Silu),
                         [("ps", b)], [("sg", tb)])
                s.op("act", lambda e: e.activation(out=warm[:, 0:1], in_=epsb[:, 0:1], func=AF.Exp), ["epsb"], ["warm"])
                for (tb, t0, n, j) in blocks:
                    b = nb()
                    for kc in range(8):
                        mm(ps[b][:, 0:n], W[:, kc, 128:256], hT[:, kc, t0:t0 + n], kc == 0, kc == 7, [wk, ("hT", kc, tb)], [("ps", b)])
                    s.op("act", lambda e, o=kT[:, t0:t0 + n], p=ps[b][:, 0:n]: e.activation(out=o, in_=p, func=AF.Identity),
                         [("ps", b)], [("kT", tb)])
                for t4 in range(0, 18, 4):
                    nt = min(4, 18 - t4)
                    b = nb()
                    for q in range(nt):
                        tt = t4 + q
                        for kc in range(8):
                            mm(ps[b][:, q * 128:(q + 1) * 128], hT[:, kc, tt * 128:(tt + 1) * 128], W[:, kc, 256:384],
                               kc == 0, kc == 7, [wk, ("hT", kc, tt // 4)], [("ps", b)])
                    s.op("dve", lambda e, o=Vt[:, t4:t4 + nt, :], p=ps[b][:, 0:nt * 128].rearrange("p (a b) -> p a b", b=128):
                         e.tensor_copy(out=o, in_=p), [("ps", b)], [("Vt", t4 // 4)])
                steps = []
                for qb in range(4):
                    tiles = [(16, 0, 512, []), (17, 0, 512, [])] + tiles_by_qb[qb]
                    for idx, (kt, c0, c1, runs) in enumerate(tiles):
                        steps.append((qb, kt, c0, c1, runs, idx == 0, idx == len(tiles) - 1))
                ns = len(steps)
                LOOK = 2
                for n in range(ns + LOOK):
                    if n < ns:
                        qb, kt, c0, c1, runs, first, last = steps[n]
                        q0 = qb * 512
                        nq = c1 - c0
                        P = Pb[n % 3]
                        pk = ("P", n % 3)
                        b0 = (4, 6)[n % 2]
                        kkey = ("kT", kt // 4)
                        nr = len(runs)
                        mm(ps[b0][:, 0:nq], kT[0:64, kt * 128:(kt + 1) * 128], qT[0:64, q0 + c0:q0 + c1], True, nr == 0,
                           [kkey, ("qT", qb)], [("ps", b0), ("ps", b0 + 1)])
                        mm(ps[b0 + 1][:, 0:nq], kT[64:128, kt * 128:(kt + 1) * 128], qT[64:128, q0 + c0:q0 + c1], True, nr == 0,
                           [kkey, ("qT", qb)], [("ps", b0), ("ps", b0 + 1)])
                        for ri, (tbl, u0, ca, cb) in enumerate(runs):
                            for hh in range(2):
                                for hf in range(2):
                                    mm(ps[b0 + hh][hf * 64:(hf + 1) * 64, ca - c0:cb - c0], ident_b[:, hf * 64:(hf + 1) * 64],
                                       EB[:, hh, tbl, u0 * 64:u0 * 64 + (cb - ca)],
                                       False, ri == nr - 1, ["EB", "ident_b"], [("ps", b0), ("ps", b0 + 1)])
                        s.op("act", lambda e, o=P[:, :, 0:nq], p=psall[:, b0:b0 + 2, 0:nq]: e.activation(out=o, in_=p, func=AF.Exp),
                             [("ps", b0), ("ps", b0 + 1)], [pk])
                    m_ = n - LOOK
                    if m_ >= 0:
                        qb_, kt_, c0_, c1_, runs_, first_, last_ = steps[m_]
                        P_ = Pb[m_ % 3]
                        pk_ = ("P", m_ % 3)
                        bo, bl = ((0, 1), (2, 3))[(pr * 4 + qb_) % 2]
                        vk = ("Vt", kt_ // 4)
                        nq_ = c1_ - c0_
                        mm(ps[bo][0:64, c0_:c1_], Vt[:, kt_, 0:64], P_[:, 0, 0:nq_], first_, last_, [vk, pk_], [("ps", bo)])
                        mm(ps[bo][64:128, c0_:c1_], Vt[:, kt_, 64:128], P_[:, 1, 0:nq_], first_, last_, [vk, pk_], [("ps", bo)])
                        mm(ps[bl][0:64, c0_:c1_], ones_1[:, :], P_[:, 0, 0:nq_], first_, last_, ["ones_1", pk_], [("ps", bl)])
                        mm(ps[bl][64:128, c0_:c1_], ones_1[:, :], P_[:, 1, 0:nq_], first_, last_, ["ones_1", pk_], [("ps", bl)])
                        if last_:
                            q0_ = qb_ * 512
                            rlb = rl[(pr * 4 + qb_) % 2]
                            rk = ("rl", (pr * 4 + qb_) % 2)
                            s.op("dve", lambda e, p=ps[bl][:, :], r=rlb: e.reciprocal(out=r[:, :], in_=p), [("ps", bl)], [rk])
                            s.op("dve", lambda e, p=ps[bo][:, :], r=rlb: e.tensor_tensor(out=r[:, :], in0=p, in1=r[:, :], op=ALU.mult),
                                 [("ps", bo), rk], [rk])
                            s.op("pool", lambda e, o=zT[:, pr, q0_:q0_ + 512], g_=sg[:, q0_:q0_ + 512], r=rlb:
                                 e.tensor_tensor(out=o, in0=r[:, :], in1=g_, op=ALU.mult), [rk, ("sg", qb_)], [("zT", pr, qb_)])
                bank[0] = 4
                if nxt:
                    for _ in range(_ada_after("na", 8)[pr]):
                        adaln_slab(i + 1, ada_n)
                        ada_n += 1
            if nxt:
                adaln_finish(i + 1)

        for i, (kind, jj) in enumerate(plan):
            blocks = blocks_for(i <= 1)
            nxt = i + 1 < len(plan)
            if kind == "pool":
                pool_layer(i, jj, blocks, nxt)
            elif kind == "na":
                na_layer(i, blocks, nxt)
            else:
                conv_layer(i, blocks, nxt)
            s.barrier_all(skip=("pe",))
            wb = list(blocks if i == 0 else blocks[:4])
            if i == len(plan) - 1:
                wb = wb[:3] + [(3, 1536, 256, 0), (3, 1792, 256, 0)]
            wout_phase(i, wb, kind, jj)
            s.barrier_all(skip=("pe",))
        out_ops = fin["out_ops"]
        s.wait_ops("sp", out_ops)
        s.emit(st)
    return nc


def _n_slabs(n_layers):
    return len(_stream_plan(n_layers))


def prep_inputs(inp, n_layers=4):
    inp = {k: np.asarray(v) for k, v in inp.items()}
    ws = _build_wstream(inp, n_layers)
    vecs = _vecs(inp)
    ident = np.eye(128, dtype=np.float32)
    band = _band_consts().reshape(128, -1)
    eb = _ebias(inp["na_rpb"][0])
    wgrp = _wgrp(inp)
    cctx = inp["c_ctx"].reshape(8, 128).T
    maps = []
    for b in range(8):
        cond = np.zeros((128, 16), np.float32)
        cond[:, 0::2] = inp["c"][b].reshape(8, 128).T
        cond[:, 1::2] = cctx
        xT_h = np.ascontiguousarray(inp["x"][b].T.reshape(8, 128, T).transpose(1, 0, 2)).reshape(128, 8 * T)
        cT_h = np.ascontiguousarray(inp["ctx"][b].T.reshape(8, 128, TC).transpose(1, 0, 2)).reshape(128, 8 * TC)
        maps.append({"wgrp": wgrp, "xT": xT_h, "ctxT": cT_h,
                     "cond": cond, "vecs": vecs, "ident": ident, "band": band,
                     "wstream": ws, "ebias": eb})
    return maps


def kernel(**inputs):
    maps = prep_inputs(inputs, 4)
    nc = build(4, _n_slabs(4))
    res = run_bass_kernel_spmd(nc, maps, core_ids=list(range(8)))
    outs = []
    for r in res.results:
        oT = np.asarray(r["outT"], dtype=np.float32).reshape(128, 8, T)
        outs.append(np.ascontiguousarray(oT.transpose(2, 1, 0).reshape(T, D)))
    return np.stack(outs, axis=0)
```

```python
import contextlib
import numpy as np
import concourse.bass as bass
import concourse.mybir as mybir
from concourse.bass_utils import run_bass_kernel_spmd

F32 = mybir.dt.float32
BF16 = mybir.dt.bfloat16
ALU = mybir.AluOpType
AF = mybir.ActivationFunctionType

ENGS = ("pe", "act", "dve", "pool", "sp")
D = 1024
T = 2048
TC = 256
TT = T + TC
NSLOT = 3
EPS = 1e-6
POOL_WINDOWS = (2, 4, 8, 16)


class Sched:
    def __init__(self, nc):
        self.nc = nc
        self.ops = []
        self.last_w = {}
        self.readers = {}
        self.eng_ops = {e: [] for e in ENGS}

    def _add(self, eng, fn, reads, writes, dma_slot=None):
        idx = len(self.ops)
        deps = set()
        for k in reads:
            w = self.last_w.get(k)
            if w is not None:
                deps.add(w)
        for k in writes:
            w = self.last_w.get(k)
            if w is not None:
                deps.add(w)
            for r in self.readers.get(k, ()):
                deps.add(r)
        for k in writes:
            self.last_w[k] = idx
            self.readers[k] = []
        for k in reads:
            self.readers.setdefault(k, []).append(idx)
        deps.discard(idx)
        self.ops.append(dict(eng=eng, fn=fn, deps=deps, slot=dma_slot,
                             pos=len(self.eng_ops[eng])))
        self.eng_ops[eng].append(idx)
        return idx

    def op(self, eng, fn, reads=(), writes=()):
        return self._add(eng, fn, tuple(reads), tuple(writes))

    def dma(self, queue, out, in_, reads=(), writes=(), slot=None):
        fn = lambda e, o=out, i=in_: e.dma_start(out=o, in_=i)
        return self._add(queue, fn, tuple(reads), tuple(writes), dma_slot=slot)

    def barrier_all(self, skip=()):
        last = []
        for e in ENGS:
            real = [i for i in self.eng_ops[e] if self.ops[i]["fn"] is not None]
            if real:
                last.append(real[-1])
        for e in ENGS:
            if e in skip:
                continue
            idx = self._add(e, None, (), ())
            self.ops[idx]["deps"] = set(last)

    def wait_ops(self, eng, dep_ops):
        idx = self._add(eng, None, (), ())
        self.ops[idx]["deps"] = set(dep_ops)

    def _skip(self, o, od):
        if od["slot"] is not None:
            return False
        if od["eng"] != o["eng"]:
            return False
        if o["eng"] == "pe":
            return True
        if o["eng"] == "pool":
            return False
        return o["pos"] - od["pos"] > 2

    def emit(self, stack):
        nc = self.nc
        ops = self.ops
        need = [False] * len(ops)
        for o in ops:
            for d in o["deps"]:
                od = ops[d]
                if od["slot"] is None and not self._skip(o, od):
                    need[d] = True
        sem_eng = {e: stack.enter_context(nc.semaphore("s_" + e)) for e in ENGS}
        slot_sems, slot_cnt, sig = {}, {}, {}
        cnt = {e: 0 for e in ENGS}
        for i, o in enumerate(ops):
            if o["slot"] is not None:
                sname = "d_" + "_".join(str(x) for x in (o["slot"] if isinstance(o["slot"], tuple) else (o["slot"],)))
                if sname not in slot_sems:
                    slot_sems[sname] = stack.enter_context(nc.semaphore(sname))
                    slot_cnt[sname] = 0
                slot_cnt[sname] += 16
                sig[i] = (slot_sems[sname], slot_cnt[sname], sname)
            elif need[i]:
                cnt[o["eng"]] += 1
                sig[i] = (sem_eng[o["eng"]], cnt[o["eng"]], "s_" + o["eng"])
        block = stack.enter_context(nc.Block())

        def run(ename, e):
            waited = {}
            for i in self.eng_ops[ename]:
                o = ops[i]
                wl = {}
                for d in o["deps"]:
                    od = ops[d]
                    if self._skip(o, od):
                        continue
                    sem, val, nm = sig[d]
                    if wl.get(nm, (None, 0))[1] < val:
                        wl[nm] = (sem, val)
                for nm, (sem, val) in wl.items():
                    if waited.get(nm, 0) >= val:
                        continue
                    waited[nm] = val
                    e.wait_ge(sem, val)
                if o["fn"] is None:
                    continue
                ins = o["fn"](e)
                if o["slot"] is not None:
                    ins.then_inc(sig[i][0], 16)
                elif need[i]:
                    ins.then_inc(sig[i][0], 1)

        @block.tensor
        def _(e):
            run("pe", e)

        @block.scalar
        def _(e):
            run("act", e)

        @block.vector
        def _(e):
            run("dve", e)

        @block.gpsimd
        def _(e):
            run("pool", e)

        @block.sync
        def _(e):
            run("sp", e)


def _slabs_from_cols(w, col_groups):
    out = []
    wr = w.reshape(8, 128, -1)
    for cols in col_groups:
        parts = [wr[:, :, c * 128:(c + 1) * 128] for c in cols]
        sl = np.concatenate(parts, axis=2)
        out.append(np.ascontiguousarray(sl.transpose(1, 0, 2)))
    return out


def _layer_plan():
    return [("pool", 0), ("na", 0), ("conv", 0), ("pool", 1)]


def _ada_after(kind, n_units):
    if kind == "pool":
        return [2, 1, 2, 1]
    return [1, 1, 1, 1, 1, 1, 0, 0]


def _stream_plan(n_layers):
    plan = _layer_plan()[:n_layers]
    out = [("ada", 0, s6) for s6 in range(6)]
    for i, (kind, j) in enumerate(plan):
        nxt = i + 1 < len(plan)
        nu = 4 if kind == "pool" else 8
        aft = _ada_after(kind, nu)
        a = 0
        for u in range(nu):
            out.append(("win", kind, j, u))
            if nxt:
                for _ in range(aft[u]):
                    out.append(("ada", i + 1, a))
                    a += 1
        out.append(("wout", kind, j, 0))
        out.append(("wout", kind, j, 1))
    return out


def _build_wstream(inp, n_layers):
    cache = {}

    def get(desc):
        if desc[0] == "ada":
            key = ("ada", desc[1])
            if key not in cache:
                cache[key] = _slabs_from_cols(inp["ada_w"][desc[1]], [[4 * q, 4 * q + 1, 4 * q + 2, 4 * q + 3] for q in range(6)])
            return cache[key][desc[2]]
        if desc[0] == "grp":
            g = np.zeros((128, 8, 512), np.float32)
            wg = inp["pool_w_grp"][desc[1]]
            for gi in range(4):
                for cc in range(2):
                    g[:, gi * 2 + cc, 0:256] = wg[gi, cc * 128:(cc + 1) * 128, :]
            return g
        if desc[0] == "win":
            _, kind, j, u = desc
            key = ("win", kind, j)
            if key not in cache:
                if kind == "pool":
                    cache[key] = _slabs_from_cols(inp["pool_w_in"][j], [[2 * gi, 2 * gi + 1, 8 + 2 * gi, 8 + 2 * gi + 1] for gi in range(4)])
                else:
                    w = inp["na_w_in"][j] if kind == "na" else inp["conv_w_in"][j]
                    cache[key] = _slabs_from_cols(w, [[c, 8 + c, 16 + c, 24 + c] for c in range(8)])
            return cache[key][u]
        _, kind, j, h = desc
        wo = {"pool": inp["pool_w_out"], "na": inp["na_w_out"], "conv": inp["conv_w_out"]}[kind][j]
        return _slabs_from_cols(wo, [[0, 1, 2, 3], [4, 5, 6, 7]])[h]

    plan = _stream_plan(n_layers)
    return np.stack([get(d) for d in plan]).reshape(len(plan), 128, 4096)


def _band_consts():
    out = np.zeros((128, 4, 5, 128), np.float32)
    L = 512
    for wi, w in enumerate(POOL_WINDOWS):
        t = np.arange(L)
        lo = np.clip(t - w // 2, 0, L)
        hi = np.clip(t + w // 2, 0, L)
        cnt = (hi - lo).astype(np.float64)
        M = np.zeros((L, L), np.float64)
        for d in range(L):
            M[lo[d]:hi[d], d] = 1.0 / cnt[d]
            M[d, d] -= 1.0
        out[:, wi, 0, :] = M[0:128, 0:128]
        out[:, wi, 1, :] = M[128:256, 128:256]
        out[:, wi, 2, :] = M[L - 128:L, L - 128:L]
        out[:, wi, 3, :] = M[0:128, 128:256]
        out[:, wi, 4, :] = M[128:256, 0:128]
    return out


def _ebias(rpb):
    p = np.arange(128)
    a = p // 64
    kc = p % 64
    u = np.arange(16)
    c = np.arange(64)
    dr = 14 - u[None, :] + a[:, None]
    cs = np.clip(c - 8, 0, 48)
    colvalid = (kc[:, None] >= cs[None, :]) & (kc[:, None] < cs[None, :] + 16)
    dc = np.clip(kc[:, None] - c[None, :] + 15, 0, 30)
    out = np.full((8, 128, 2, 2, 16, 64), -200.0, np.float32)
    for tbl in range(2):
        drvalid = (dr >= 0) & (dr <= 14)
        if tbl == 1:
            drvalid &= (dr >= 3) & (dr <= 10)
        valid = drvalid[:, :, None] & colvalid[:, None, :]
        drc = np.clip(dr, 0, 14)
        for h in range(16):
            g = rpb[h][drc[:, :, None], dc[:, None, :]]
            out[h // 2, :, h % 2, tbl] = np.where(valid, g, np.float32(-200.0))
    return out.reshape(8, 128, 4096)


def _wgrp(inp):
    g = np.zeros((2, 128, 8, 256), np.float32)
    for j in range(2):
        wg = inp["pool_w_grp"][j]
        for gi in range(4):
            for cc in range(2):
                g[j, :, gi * 2 + cc, :] = wg[gi, cc * 128:(cc + 1) * 128, :]
    return g.reshape(2, 128, 2048)


def _vecs(inp):
    v = np.zeros((128, 184), np.float32)
    v[:, 176:184] = inp["final_g"].reshape(8, 128).T
    for i in range(4):
        v[:, i * 8:(i + 1) * 8] = inp["norm_g"][i].reshape(8, 128).T
        v[:, 32 + i * 24:32 + (i + 1) * 24] = inp["ada_b"][i].reshape(24, 128).T
    for j in range(2):
        v[:, 128 + j * 8:128 + (j + 1) * 8] = inp["pool_scale"][j].reshape(8, 128).T
    for tap in range(3):
        v[:, 144 + tap * 8:144 + (tap + 1) * 8] = inp["conv_dw"][0][tap].reshape(8, 128).T
    v[:, 168:176] = inp["conv_db"][0].reshape(8, 128).T
    return v


def _na_tiles(qb):
    rows = list(range(8 * qb, 8 * qb + 8))
    need = {}
    for r in rows:
        if r <= 4:
            kts, tbl = range(0, 4), 0
        elif r >= 28:
            kts, tbl = range(12, 16), 0
        else:
            kts, tbl = range(-((-(r - 5)) // 2), (r + 3) // 2 + 1), 1
        for kt in kts:
            need.setdefault(kt, []).append((r, tbl))
    out = []
    for kt in sorted(need):
        lst = need[kt]
        rs = [r for r, _ in lst]
        assert rs == list(range(rs[0], rs[-1] + 1))
        runs = []
        start = 0
        for k in range(1, len(lst) + 1):
            if k == len(lst) or lst[k][1] != lst[start][1]:
                ra, rb, tbl = lst[start][0], lst[k - 1][0], lst[start][1]
                u0 = ra - 2 * kt + 7
                assert 0 <= u0 and u0 + (rb - ra + 1) <= 16
                runs.append((tbl, u0, (ra - 8 * qb) * 64, (rb - 8 * qb + 1) * 64))
                start = k
        out.append((kt, (rs[0] - 8 * qb) * 64, (rs[-1] - 8 * qb + 1) * 64, runs))
    return out


def build(n_layers=4, n_slabs=58, stage=99):
    n_slabs = _n_slabs(n_layers)
    nc = bass.Bass("TRN2", target_bir_lowering=False)
    x_d = nc.dram_tensor("xT", [128, 8 * T], F32, kind="ExternalInput").ap().rearrange("p (a b) -> p a b", b=T)
    ctx_d = nc.dram_tensor("ctxT", [128, 8 * TC], F32, kind="ExternalInput").ap().rearrange("p (a b) -> p a b", b=TC)
    cond_d = nc.dram_tensor("cond", [128, 16], F32, kind="ExternalInput").ap()
    vecs_d = nc.dram_tensor("vecs", [128, 184], F32, kind="ExternalInput").ap()
    ident_d = nc.dram_tensor("ident", [128, 128], F32, kind="ExternalInput").ap()
    band_d = nc.dram_tensor("band", [128, 4 * 5 * 128], F32, kind="ExternalInput").ap()
    w_d = nc.dram_tensor("wstream", [n_slabs, 128, 4096], F32, kind="ExternalInput").ap()
    eb_d = nc.dram_tensor("ebias", [8, 128, 4096], F32, kind="ExternalInput").ap()
    wgrp_d = nc.dram_tensor("wgrp", [2, 128, 2048], F32, kind="ExternalInput").ap()
    out_d = nc.dram_tensor("outT", [128, 8 * T], F32, kind="ExternalOutput").ap().rearrange("p (a b) -> p a b", b=T)

    st = contextlib.ExitStack()
    with st:
        sb = lambda n, shp, dt: st.enter_context(nc.sbuf_tensor("sb_" + n, shp, dt))
        xT = sb("xT", [128, 8, TT], F32)
        hT = sb("hT", [128, 8, TT], BF16)
        zT = sb("zT", [128, 8, TT], BF16)
        wring = [sb("wring%d" % i, [128, 8, 512], BF16) for i in range(NSLOT)]
        ident = sb("ident", [128, 128], F32)
        vecs = sb("vecs", [128, 184], F32)
        condf = sb("condf", [128, 16], F32)
        conds = sb("conds", [128, 16], BF16)
        ones_m = sb("ones_m", [128, 128], BF16)
        ones_1 = sb("ones_1", [128, 64], BF16)
        ident_b = sb("ident_b", [128, 128], BF16)
        epsb = sb("epsb", [128, 1], F32)
        warm = sb("warm", [128, 1], F32)
        mod = sb("mod", [128, 4, 24, 2], F32)
        modA = sb("modA", [128, 4, 8, 2], F32)
        tmpA = sb("tmpA", [128, 8, 2], F32)
        arena = sb("arena", [128, 9216], F32)
        psall = st.enter_context(nc.psum_tensor("psall", [128, 8, 512], F32))
        ps = [psall[:, i, :] for i in range(8)]

        zflat = zT[:, :, :].rearrange("p a b -> p (a b)").bitcast(F32)

        def carve(off_bytes, shape, dt, base=None):
            esz = 4 if dt == F32 else 2
            n = int(np.prod(shape))
            assert off_bytes % 4 == 0 and off_bytes + n * esz <= 9216 * 4, (off_bytes, shape)
            src = arena if base is None else base
            a = src[:, off_bytes // 4:(off_bytes + n * esz + 3) // 4]
            if dt != F32:
                a = a.bitcast(dt)
                a = a[:, 0:n]
            if len(shape) == 2:
                return a.rearrange("p (a b) -> p a b", b=shape[1])
            if len(shape) == 3:
                return a.rearrange("p (a b c) -> p a b c", b=shape[1], c=shape[2])
            return a

        s = Sched(nc)
        bank = [0]

        def nb():
            b = bank[0]
            bank[0] = (b + 1) % 8
            return b

        def mm(out, lhsT, rhs, start, stop, reads, writes):
            s.op("pe", lambda e, o=out, l=lhsT, r=rhs, a=start, z=stop:
                 e.matmul(o, lhsT=l, rhs=r, start=a, stop=z), reads, writes)

        wstate = dict(next_load=0, next_use=0)

        def issue_load():
            n = wstate["next_load"]
            if n >= n_slabs:
                return
            sl = n % NSLOT
            s.dma("pool", wring[sl][:], w_d[n].rearrange("p (a b) -> p a b", b=512),
                  writes=[("w", sl)], slot=("w", sl))
            wstate["next_load"] = n + 1

        splan = _stream_plan(n_layers)
        assert len(splan) == n_slabs

        def next_slab(expect, hold_prev=0):
            n = wstate["next_use"]
            assert splan[n][:len(expect)] == expect, (n, splan[n], expect)
            while wstate["next_load"] < min(n + NSLOT - hold_prev, n_slabs):
                issue_load()
            wstate["next_use"] = n + 1
            return wring[n % NSLOT], ("w", n % NSLOT)

        s.dma("sp", ident[:], ident_d, writes=["ident"], slot="c_ident")
        s.dma("sp", vecs[:], vecs_d, writes=["vecs"], slot="c_vecs")
        s.dma("sp", condf[:], cond_d, writes=["condf"], slot="c_cond")
        s.op("act", lambda e: e.activation(out=conds[:], in_=condf[:], func=AF.Silu), ["condf"], ["conds"])
        s.op("pool", lambda e: e.memset(ones_m[:], 1.0 / 1024), [], ["ones_m"])
        s.op("pool", lambda e: e.memset(ones_1[:], 1.0), [], ["ones_1"])
        s.op("dve", lambda e: e.tensor_copy(out=ident_b[:], in_=ident[:]), ["ident"], ["ident_b"])
        s.op("pool", lambda e: e.memset(epsb[:], EPS), [], ["epsb"])
        for _ in range(NSLOT):
            issue_load()

        plan = _layer_plan()[:n_layers]

        def blocks_for(with_ctx):
            bl = [(tb, tb * 512, 512, 0) for tb in range(4)]
            if with_ctx:
                bl.append((4, T, TC, 1))
            return bl

        def adaln_slab(i, s6, hold_prev=0):
            W, wk = next_slab(("ada", i, s6), hold_prev=hold_prev)
            b = nb()
            for q in range(4):
                for kc in range(8):
                    mm(ps[b][:, q * 2:q * 2 + 2], W[:, kc, q * 128:(q + 1) * 128], conds[:, kc * 2:(kc + 1) * 2], kc == 0, kc == 7,
                       [wk, "conds"], [("ps", b)])
            for j in range(2):
                s.op("dve", lambda e, o=mod[:, i, 4 * s6:4 * s6 + 4, j], p=ps[b][:, j:8:2],
                     v=vecs[:, 32 + i * 24 + 4 * s6:32 + i * 24 + 4 * s6 + 4]:
                     e.tensor_tensor(out=o, in0=p, in1=v, op=ALU.add), [("ps", b), "vecs"], [("mod", i)])

        def adaln_finish(i):
            mi = mod[:, i, :, :]
            for j in range(2):
                s.op("dve", lambda e, o=tmpA[:, :, j], a=mi[:, 8:16, j]:
                     e.tensor_scalar(out=o, in0=a, scalar1=1.0, scalar2=None, op0=ALU.add), [("mod", i)], ["tmpA"])
                s.op("dve", lambda e, o=modA[:, i, :, j], a=tmpA[:, :, j], v=vecs[:, i * 8:(i + 1) * 8]:
                     e.tensor_tensor(out=o, in0=a, in1=v, op=ALU.mult), ["tmpA", "vecs"], [("modA", i)])

        HOFF = 8192
        hbase = [None]

        def hphase_A(i, blk, part=0, kcs=0):
            (tb, t0, n, j) = blk
            sq = carve(HOFF if (hbase[0] is None or tb % 2 == 0) else 0, [8, 512], BF16, hbase[0])
            rstd = carve(HOFF + 8192 + (tb % 2) * 2048, [512], F32, hbase[0])
            rk = ("rstd", tb % 2)
            if part == 3:
                s.op("act", lambda e, o=sq[:, kcs, 0:n], i_=xT[:, kcs, t0:t0 + n]: e.activation(out=o, in_=i_, func=AF.Square),
                     [("xT", kcs, tb)], [("sq", kcs)])
                return
            if part in (0, 1):
                if hbase[0] is not None:
                    if tb == 0:
                        s.op("act", lambda e, o=sq[:, :, 0:n], i_=xT[:, :, t0:t0 + n]: e.activation(out=o, in_=i_, func=AF.Square),
                             [("xT", kc, tb) for kc in range(8)], [("sqp", tb % 2, kc) for kc in range(8)])
                    else:
                        s.op("pool", lambda e, o=sq[:, :, 0:n], i_=xT[:, :, t0:t0 + n]: e.tensor_tensor(out=o, in0=i_, in1=i_, op=ALU.mult),
                             [("xT", kc, tb) for kc in range(8)], [("sqp", tb % 2, kc) for kc in range(8)])
                else:
                    s.op("act", lambda e, o=sq[:, :, 0:n], i_=xT[:, :, t0:t0 + n]: e.activation(out=o, in_=i_, func=AF.Square),
                         [("xT", kc, tb) for kc in range(8)], [("sq", kc) for kc in range(8)])
            if part == 1:
                return
            b = nb()
            for kc in range(8):
                sk = ("sq", kc) if hbase[0] is None else ("sqp", tb % 2, kc)
                mm(ps[b][:, 0:n], ones_m[:], sq[:, kc, 0:n], kc == 0, kc == 7, [sk, "ones_m"], [("ps", b)])
            s.op("act", lambda e, o=rstd[:, 0:n], p=ps[b][:, 0:n]: e.activation(out=o, in_=p, func=AF.Ln, bias=epsb[:, 0:1], scale=1.0),
                 [("ps", b), "epsb"], [rk])
            s.op("act", lambda e, o=rstd[:, 0:n]: e.activation(out=o, in_=o, func=AF.Exp, scale=-0.5), [rk], [rk])

        def hphase_B(i, blk, kc):
            (tb, t0, n, j) = blk
            rstd = carve(HOFF + 8192 + (tb % 2) * 2048, [512], F32, hbase[0])
            rk = ("rstd", tb % 2)
            tmpb = [carve(HOFF + 8192 + 4096 + k * 2048, [512], F32, hbase[0]) for k in range(2)]
            tm = tmpb[kc % 2]
            s.op("dve", lambda e, o=tm[:, 0:n], a=xT[:, kc, t0:t0 + n], sc=modA[:, i, kc, j:j + 1], r=rstd[:, 0:n]:
                 e.scalar_tensor_tensor(out=o, in0=a, scalar=sc, in1=r, op0=ALU.mult, op1=ALU.mult),
                 [("xT", kc, tb), ("modA", i), rk], [("tmpb", kc % 2)])
            s.op("act", lambda e, o=hT[:, kc, t0:t0 + n], a=tm[:, 0:n], bi=mod[:, i, kc, j:j + 1]:
                 e.activation(out=o, in_=a, func=AF.Identity, bias=bi, scale=1.0),
                 [("tmpb", kc % 2), ("mod", i)], [("hT", kc, tb)])

        fin = dict(out_ops=[])

        def final_B(blk, kc):
            (tb, t0, n, j) = blk
            rstd = carve(HOFF + 8192 + (tb % 2) * 2048, [512], F32)
            rk = ("rstd", tb % 2)
            h = kc // 4
            ost = carve(HOFF + 12288 + h * 8192, [4, 512], F32)
            s.op("dve", lambda e, o=ost[:, kc % 4, 0:n], a=xT[:, kc, t0:t0 + n], sc=vecs[:, 176 + kc:177 + kc], r=rstd[:, 0:n]:
                 e.scalar_tensor_tensor(out=o, in0=a, scalar=sc, in1=r, op0=ALU.mult, op1=ALU.mult),
                 [("xT", kc, tb), "vecs", rk], [("ost", h)])
            if kc % 4 == 3:
                fin["out_ops"].append(s.dma("sp", out_d[:, 4 * h:4 * h + 4, t0:t0 + n], ost[:, :, 0:n],
                                            reads=[("ost", h)], slot=("out", h)))

        def hphase_block(i, blk):
            hphase_A(i, blk)
            for kc in range(8):
                hphase_B(i, blk, kc)

        def wout_phase(i, blocks, kind, jj):
            last = (i == len(plan) - 1)
            Ws = [next_slab(("wout", kind, jj, 0)), next_slab(("wout", kind, jj, 1), hold_prev=1)]
            pend = None
            for blk in blocks:
                (tb, t0, n, j) = blk
                for oc in range(8):
                    W, wk = Ws[oc // 4]
                    ocl = oc % 4
                    b = nb()
                    for kc in range(8):
                        mm(ps[b][:, 0:n], W[:, kc, ocl * 128:(ocl + 1) * 128], zT[:, kc, t0:t0 + n], kc == 0, kc == 7,
                           [wk, ("zT", kc, tb)], [("ps", b)])
                    s.op("dve", lambda e, o=xT[:, oc, t0:t0 + n], p=ps[b][:, 0:n], g=mod[:, i, 16 + oc, j:j + 1]:
                         e.scalar_tensor_tensor(out=o, in0=p, scalar=g, in1=o, op0=ALU.mult, op1=ALU.add),
                         [("ps", b), ("mod", i), ("xT", oc, tb)], [("xT", oc, tb)])
                    Bop = (lambda bk, kc_: final_B(bk, kc_)) if last else (lambda bk, kc_: hphase_B(i + 1, bk, kc_))
                    if pend is not None:
                        if oc == 0:
                            hphase_A(i + 1, pend, part=2)
                        elif oc >= 2:
                            Bop(pend, oc - 2)
                    hphase_A(i + 1, blk, part=3, kcs=oc)
                if pend is not None:
                    Bop(pend, 6)
                    Bop(pend, 7)
                pend = blk
            hphase_A(i + 1, pend, part=2)
            for kc in range(8):
                Bop(pend, kc)

        hbase[0] = zflat
        pb = blocks_for(True)
        for (tb, t0, n, j) in pb:
            src = x_d[:, :, t0:t0 + n] if tb < 4 else ctx_d[:, :, :]
            s.dma("sp", xT[:, :, t0:t0 + n], src, writes=[("xT", kc, tb) for kc in range(8)], slot=("xld", tb))
        for s6 in range(6):
            adaln_slab(0, s6, hold_prev=1 if s6 == 5 else 0)
        adaln_finish(0)
        for k in range(5):
            hphase_A(0, pb[k])
            if k >= 1:
                for kc in range(8):
                    hphase_B(0, pb[k - 1], kc)
        for kc in range(8):
            hphase_B(0, pb[4], kc)
        hbase[0] = None

        def hkeys(tb):
            return [("hT", kc, tb) for kc in range(8)]

        def pool_layer(i, jj, blocks, nxt):
            ntile = 18 if len(blocks) == 5 else 16
            utok = carve(0, [18, 256], BF16)
            dT = carve(9216, [2, TT], BF16)
            sg = carve(18432, [2, TT], BF16)
            bandb = carve(27648, [20, 128], BF16)
            wg = carve(32768, [8, 256], BF16)
            s.dma("pool", bandb, band_d.rearrange("p (a b) -> p a b", b=128), writes=["bandb"], slot="bandb")
            s.dma("pool", wg, wgrp_d[jj].rearrange("p (a b) -> p a b", b=256), writes=["wg"], slot="wg")
            ada_n = 0
            for g in range(4):
                W, wk = next_slab(("win", "pool", jj, g))
                for t2 in range(0, ntile, 2):
                    b = nb()
                    for hf in range(2):
                        tt = t2 + hf
                        for kc in range(8):
                            mm(ps[b][:, hf * 256:(hf + 1) * 256], hT[:, kc, tt * 128:(tt + 1) * 128], W[:, kc, 0:256],
                               kc == 0, kc == 7, [wk, ("hT", kc, tt // 4)], [("ps", b)])
                    s.op("act", lambda e, o=utok[:, t2:t2 + 2, :], p=ps[b][:, :].rearrange("p (a b) -> p a b", b=256):
                         e.activation(out=o, in_=p, func=AF.Identity), [("ps", b)], [("utok", t2 // 4)])
                for cc in range(2):
                    for (tb, t0, n, j) in blocks:
                        b = nb()
                        for kc in range(8):
                            mm(ps[b][:, 0:n], W[:, kc, 256 + cc * 128:256 + (cc + 1) * 128], hT[:, kc, t0:t0 + n],
                               kc == 0, kc == 7, [wk, ("hT", kc, tb)], [("ps", b)])
                        s.op("act", lambda e, o=sg[:, cc, t0:t0 + n], p=ps[b][:, 0:n]: e.activation(out=o, in_=p, func=AF.Silu),
                             [("ps", b)], [("sg", cc, tb)])
                for cc in range(2):
                    for (tb, t0, n, j) in blocks:
                        b = nb()
                        for dl in range(n // 128):
                            td = t0 // 128 + dl
                            seg0, seg1 = (0, 16) if td < 16 else (16, 18)
                            srcs = []
                            if td - 1 >= seg0:
                                srcs.append((td - 1, 3))
                            srcs.append((td, 0 if td == seg0 else (2 if td == seg1 - 1 else 1)))
                            if td + 1 < seg1:
                                srcs.append((td + 1, 4))
                            for k, (ts, ty) in enumerate(srcs):
                                mm(ps[b][:, dl * 128:(dl + 1) * 128], utok[:, ts, cc * 128:(cc + 1) * 128],
                                   bandb[:, g * 5 + ty, :], k == 0, k == len(srcs) - 1,
                                   [("utok", ts // 4), "bandb"], [("ps", b)])
                        s.op("dve", lambda e, o=dT[:, cc, t0:t0 + n], p=ps[b][:, 0:n]: e.tensor_copy(out=o, in_=p),
                             [("ps", b)], [("dT", cc, tb)])
                for dc in range(2):
                    for (tb, t0, n, j) in blocks:
                        b = nb()
                        for cc in range(2):
                            mm(ps[b][:, 0:n], wg[:, g * 2 + cc, dc * 128:(dc + 1) * 128], dT[:, cc, t0:t0 + n],
                               cc == 0, cc == 1, ["wg", ("dT", cc, tb)], [("ps", b)])
                        ch = 2 * g + dc
                        s.op("dve", lambda e, o=zT[:, ch, t0:t0 + n], p=ps[b][:, 0:n], sc=vecs[:, 128 + jj * 8 + ch:128 + jj * 8 + ch + 1],
                             g_=sg[:, dc, t0:t0 + n]:
                             e.scalar_tensor_tensor(out=o, in0=p, scalar=sc, in1=g_, op0=ALU.mult, op1=ALU.mult),
                             [("ps", b), "vecs", ("sg", dc, tb)], [("zT", ch, tb)])
                if nxt:
                    for _ in range(_ada_after("pool", 4)[g]):
                        adaln_slab(i + 1, ada_n)
                        ada_n += 1
            if nxt:
                adaln_finish(i + 1)

        def conv_layer(i, blocks, nxt):
            vs = carve(0, [T], F32)
            pp = carve(8192, [T + 2], F32)
            acc = carve(16400, [T], F32)
            sg = carve(24592, [T], BF16)
            s.op("pool", lambda e: e.memset(pp[:, 0:1], 0.0), [], ["pp_pad"])
            s.op("pool", lambda e: e.memset(pp[:, T + 1:T + 2], 0.0), [], ["pp_pad"])
            ada_n = 0
            for c in range(8):
                W, wk = next_slab(("win", "conv", 0, c))

                def part_mm(part, tb, t0):
                    b = nb()
                    for kc in range(8):
                        mm(ps[b][:, :], W[:, kc, part * 128:(part + 1) * 128], hT[:, kc, t0:t0 + 512], kc == 0, kc == 7,
                           [wk, ("hT", kc, tb)], [("ps", b)])
                    return b
                for (tb, t0, n, j) in blocks:
                    b = part_mm(2, tb, t0)
                    s.op("act", lambda e, o=vs[:, t0:t0 + 512], p=ps[b][:, :]: e.activation(out=o, in_=p, func=AF.Identity),
                         [("ps", b)], [("vs", tb)])
                for (tb, t0, n, j) in blocks:
                    b = part_mm(1, tb, t0)
                    s.op("dve", lambda e, o=pp[:, 1 + t0:1 + t0 + 512], p=ps[b][:, :], v=vs[:, t0:t0 + 512]:
                         e.tensor_tensor(out=o, in0=p, in1=v, op=ALU.mult), [("ps", b), ("vs", tb)], ["pp"])
                w0 = vecs[:, 144 + c:144 + c + 1]
                w1 = vecs[:, 152 + c:152 + c + 1]
                w2 = vecs[:, 160 + c:160 + c + 1]
                bb = vecs[:, 168 + c:168 + c + 1]
                s.op("pool", lambda e, a=w1, b_=bb: e.tensor_scalar(out=acc[:, :], in0=pp[:, 1:T + 1], scalar1=a, scalar2=b_,
                                                                   op0=ALU.mult, op1=ALU.add), ["pp", "pp_pad", "vecs"], ["acc"])
                s.op("dve", lambda e, a=w0: e.scalar_tensor_tensor(out=acc[:, :], in0=pp[:, 0:T], scalar=a, in1=acc[:, :],
                                                                   op0=ALU.mult, op1=ALU.add), ["pp", "pp_pad", "acc"], ["acc"])
                s.op("dve", lambda e, a=w2: e.scalar_tensor_tensor(out=acc[:, :], in0=pp[:, 2:T + 2], scalar=a, in1=acc[:, :],
                                                                   op0=ALU.mult, op1=ALU.add), ["pp", "pp_pad", "acc"], ["acc"])
                for (tb, t0, n, j) in blocks:
                    b = part_mm(3, tb, t0)
                    s.op("act", lambda e, o=sg[:, t0:t0 + 512], p=ps[b][:, :]: e.activation(out=o, in_=p, func=AF.Silu),
                         [("ps", b)], [("sg", tb)])
                for (tb, t0, n, j) in blocks:
                    b = part_mm(0, tb, t0)
                    s.op("dve", lambda e, o=acc[:, t0:t0 + 512], p=ps[b][:, :]: e.tensor_tensor(out=o, in0=p, in1=o, op=ALU.mult),
                         [("ps", b), "acc"], [("y", tb)])
                    s.op("pool", lambda e, o=zT[:, c, t0:t0 + 512], a=acc[:, t0:t0 + 512], g_=sg[:, t0:t0 + 512]:
                         e.tensor_tensor(out=o, in0=a, in1=g_, op=ALU.mult), [("y", tb), ("sg", tb)], [("zT", c, tb), "acc"])
                if nxt:
                    for _ in range(_ada_after("conv", 8)[c]):
                        adaln_slab(i + 1, ada_n)
                        ada_n += 1
            if nxt:
                adaln_finish(i + 1)

        def na_layer(i, blocks, nxt):
            qT = carve(0, [T], BF16)
            kT = carve(4096, [TT], BF16)
            Vt = carve(8704, [18, 128], BF16)
            sg = carve(13312, [T], BF16)
            EB = carve(17408, [2, 2, 1024], BF16)
            rl = [carve(25600 + k * 2048, [512], F32) for k in range(2)]
            Pb = [carve(29696 + k * 2048, [2, 512], BF16) for k in range(3)]
            tiles_by_qb = [_na_tiles(qb) for qb in range(4)]
            ada_n = 0
            for pr in range(8):
                W, wk = next_slab(("win", "na", 0, pr))
                s.dma("pool", EB, eb_d[pr].rearrange("p (a b c) -> p a b c", a=2, b=2), writes=["EB"], slot="EB")
                for (tb, t0, n, j) in blocks[:4]:
                    b = nb()
                    for kc in range(8):
                        mm(ps[b][:, :], W[:, kc, 0:128], hT[:, kc, t0:t0 + 512], kc == 0, kc == 7, [wk, ("hT", kc, tb)], [("ps", b)])
                    s.op("dve", lambda e, o=qT[:, t0:t0 + 512], p=ps[b][:, :]:
                         e.tensor_scalar(out=o, in0=p, scalar1=0.125, scalar2=None, op0=ALU.mult), [("ps", b)], [("qT", tb)])
                for (tb, t0, n, j) in blocks[:4]:
                    b = nb()
                    for kc in range(8):
                        mm(ps[b][:, :], W[:, kc, 384:512], hT[:, kc, t0:t0 + 512], kc == 0, kc == 7, [wk, ("hT", kc, tb)], [("ps", b)])
                    s.op("act", lambda e, o=sg[:, t0:t0 + 512], p=ps[b][:, :]: e.activation(out=o, in_=p, func=AF.Silu),
                         [("ps", b)], [("sg", tb)])
                s.op("act", lambda e: e.activation(out=warm[:, 0:1], in_=epsb[:, 0:1], func=AF.Exp), ["epsb"], ["warm"])
                for (tb, t0, n, j) in blocks:
                    b = nb()
                    for kc in range(8):
                        mm(ps[b][:, 0:n], W[:, kc, 128:256], hT[:, kc, t0:t0 + n], kc == 0, kc == 7, [wk, ("hT", kc, tb)], [("ps", b)])
                    s.op("act", lambda e, o=kT[:, t0:t0 + n], p=ps[b][:, 0:n]: e.activation(out=o, in_=p, func=AF.Identity),
                         [("ps", b)], [("kT", tb)])
                for t4 in range(0, 18, 4):
                    nt = min(4, 18 - t4)
                    b = nb()
                    for q in range(nt):
                        tt = t4 + q
                        for kc in range(8):
                            mm(ps[b][:, q * 128:(q + 1) * 128], hT[:, kc, tt * 128:(tt + 1) * 128], W[:, kc, 256:384],
                               kc == 0, kc == 7, [wk, ("hT", kc, tt // 4)], [("ps", b)])
                    s.op("dve", lambda e, o=Vt[:, t4:t4 + nt, :], p=ps[b][:, 0:nt * 128].rearrange("p (a b) -> p a b", b=128):
                         e.tensor_copy(out=o, in_=p), [("ps", b)], [("Vt", t4 // 4)])
                steps = []
                for qb in range(4):
                    tiles = [(16, 0, 512, []), (17, 0, 512, [])] + tiles_by_qb[qb]
                    for idx, (kt, c0, c1, runs) in enumerate(tiles):
                        steps.append((qb, kt, c0, c1, runs, idx == 0, idx == len(tiles) - 1))
                ns = len(steps)
                LOOK = 2
                for n in range(ns + LOOK):
                    if n < ns:
                        qb, kt, c0, c1, runs, first, last = steps[n]
                        q0 = qb * 512
                        nq = c1 - c0
                        P = Pb[n % 3]
                        pk = ("P", n % 3)
                        b0 = (4, 6)[n % 2]
                        kkey = ("kT", kt // 4)
                        nr = len(runs)
                        mm(ps[b0][:, 0:nq], kT[0:64, kt * 128:(kt + 1) * 128], qT[0:64, q0 + c0:q0 + c1], True, nr == 0,
                           [kkey, ("qT", qb)], [("ps", b0), ("ps", b0 + 1)])
                        mm(ps[b0 + 1][:, 0:nq], kT[64:128, kt * 128:(kt + 1) * 128], qT[64:128, q0 + c0:q0 + c1], True, nr == 0,
                           [kkey, ("qT", qb)], [("ps", b0), ("ps", b0 + 1)])
                        for ri, (tbl, u0, ca, cb) in enumerate(runs):
                            for hh in range(2):
                                for hf in range(2):
                                    mm(ps[b0 + hh][hf * 64:(hf + 1) * 64, ca - c0:cb - c0], ident_b[:, hf * 64:(hf + 1) * 64],
                                       EB[:, hh, tbl, u0 * 64:u0 * 64 + (cb - ca)],
                                       False, ri == nr - 1, ["EB", "ident_b"], [("ps", b0), ("ps", b0 + 1)])
                        s.op("act", lambda e, o=P[:, :, 0:nq], p=psall[:, b0:b0 + 2, 0:nq]: e.activation(out=o, in_=p, func=AF.Exp),
                             [("ps", b0), ("ps", b0 + 1)], [pk])
                    m_ = n - LOOK
                    if m_ >= 0:
                        qb_, kt_, c0_, c1_, runs_, first_, last_ = steps[m_]
                        P_ = Pb[m_ % 3]
                        pk_ = ("P", m_ % 3)
                        bo, bl = ((0, 1), (2, 3))[(pr * 4 + qb_) % 2]
                        vk = ("Vt", kt_ // 4)
                        nq_ = c1_ - c0_
                        mm(ps[bo][0:64, c0_:c1_], Vt[:, kt_, 0:64], P_[:, 0, 0:nq_], first_, last_, [vk, pk_], [("ps", bo)])
                        mm(ps[bo][64:128, c0_:c1_], Vt[:, kt_, 64:128], P_[:, 1, 0:nq_], first_, last_, [vk, pk_], [("ps", bo)])
                        mm(ps[bl][0:64, c0_:c1_], ones_1[:, :], P_[:, 0, 0:nq_], first_, last_, ["ones_1", pk_], [("ps", bl)])
                        mm(ps[bl][64:128, c0_:c1_], ones_1[:, :], P_[:, 1, 0:nq_], first_, last_, ["ones_1", pk_], [("ps", bl)])
                        if last_:
                            q0_ = qb_ * 512
                            rlb = rl[(pr * 4 + qb_) % 2]
                            rk = ("rl", (pr * 4 + qb_) % 2)
                            s.op("dve", lambda e, p=ps[bl][:, :], r=rlb: e.reciprocal(out=r[:, :], in_=p), [("ps", bl)], [rk])
                            s.op("dve", lambda e, p=ps[bo][:, :], r=rlb: e.tensor_tensor(out=r[:, :], in0=p, in1=r[:, :], op=ALU.mult),
                                 [("ps", bo), rk], [rk])
                            s.op("pool", lambda e, o=zT[:, pr, q0_:q0_ + 512], g_=sg[:, q0_:q0_ + 512], r=rlb:
                                 e.tensor_tensor(out=o, in0=r[:, :], in1=g_, op=ALU.mult), [rk, ("sg", qb_)], [("zT", pr, qb_)])
                bank[0] = 4
                if nxt:
                    for _ in range(_ada_after("na", 8)[pr]):
                        adaln_slab(i + 1, ada_n)
                        ada_n += 1
            if nxt:
                adaln_finish(i + 1)

        for i, (kind, jj) in enumerate(plan):
            blocks = blocks_for(i <= 1)
            nxt = i + 1 < len(plan)
            if kind == "pool":
                pool_layer(i, jj, blocks, nxt)
            elif kind == "na":
                na_layer(i, blocks, nxt)
            else:
                conv_layer(i, blocks, nxt)
            s.barrier_all(skip=("pe",))
            wb = list(blocks if i == 0 else blocks[:4])
            if i == len(plan) - 1:
                wb = wb[:3] + [(3, 1536, 256, 0), (3, 1792, 256, 0)]
            wout_phase(i, wb, kind, jj)
            s.barrier_all(skip=("pe",))
        out_ops = fin["out_ops"]
        s.wait_ops("sp", out_ops)
        s.emit(st)
    return nc


def _n_slabs(n_layers):
    return len(_stream_plan(n_layers))


def prep_inputs(inp, n_layers=4):
    inp = {k: np.asarray(v) for k, v in inp.items()}
    ws = _build_wstream(inp, n_layers)
    vecs = _vecs(inp)
    ident = np.eye(128, dtype=np.float32)
    band = _band_consts().reshape(128, -1)
    eb = _ebias(inp["na_rpb"][0])
    wgrp = _wgrp(inp)
    cctx = inp["c_ctx"].reshape(8, 128).T
    maps = []
    for b in range(8):
        cond = np.zeros((128, 16), np.float32)
        cond[:, 0::2] = inp["c"][b].reshape(8, 128).T
        cond[:, 1::2] = cctx
        xT_h = np.ascontiguousarray(inp["x"][b].T.reshape(8, 128, T).transpose(1, 0, 2)).reshape(128, 8 * T)
        cT_h = np.ascontiguousarray(inp["ctx"][b].T.reshape(8, 128, TC).transpose(1, 0, 2)).reshape(128, 8 * TC)
        maps.append({"wgrp": wgrp, "xT": xT_h, "ctxT": cT_h,
                     "cond": cond, "vecs": vecs, "ident": ident, "band": band,
                     "wstream": ws, "ebias": eb})
    return maps


def kernel(**inputs):
    maps = prep_inputs(inputs, 4)
    nc = build(4, _n_slabs(4))
    res = run_bass_kernel_spmd(nc, maps, core_ids=list(range(8)))
    outs = []
    for r in res.results:
        oT = np.asarray(r["outT"], dtype=np.float32).reshape(128, 8, T)
        outs.append(np.ascontiguousarray(oT.transpose(2, 1, 0).reshape(T, D)))
    return np.stack(outs, axis=0)
```
